# Optimizing a Trainium2 kernel written in Bass

```python
import math
import jax, jax.numpy as jnp
from jax import lax
import numpy as np

D_MODEL = 1024
BATCH = 8
SEQ = 2048
DEPTH = 2

CHUNK = 64
Q_BLOCK = 128
DIFF_HEADS = D_MODEL // 256
DIFF_HEAD_DIM = 64
DIFF_WIDTH = DIFF_HEADS * 2 * DIFF_HEAD_DIM
MLA_HEADS = D_MODEL // 256
MLA_NOPE_DIM = 128
MLA_ROPE_DIM = 64
MLA_V_DIM = 128
MLA_Q_RANK = D_MODEL // 4
MLA_KV_RANK = D_MODEL // 8
MLA_WIDTH = MLA_HEADS * MLA_V_DIM
MIX_WIDTH = DIFF_WIDTH + MLA_WIDTH
IN_WIDTH = 3 * DIFF_WIDTH + MLA_Q_RANK + MLA_KV_RANK + MLA_ROPE_DIM
D_FF = 4 * D_MODEL
N_BUCKETS = 32
MAX_DISTANCE = 128
ROPE_THETA = 10000.0
ALPHA = (2 * DEPTH) ** 0.25
BETA = (8 * DEPTH) ** -0.25
LN_EPS = 1e-5
RMS_EPS = 1e-6

kernel_name = "hybrid_diffattn_mla_deepnorm_encoder"


def _layer_norm(x, g, b):
    xf = x.astype(jnp.float32)
    mu = jnp.mean(xf, axis=-1, keepdims=True)
    var = jnp.mean(jnp.square(xf - mu), axis=-1, keepdims=True)
    y = (xf - mu) * lax.rsqrt(var + LN_EPS)
    return (y * g.astype(jnp.float32) + b.astype(jnp.float32)).astype(x.dtype)


def _rms_norm(x, g):
    xf = x.astype(jnp.float32)
    y = xf * lax.rsqrt(jnp.mean(jnp.square(xf), axis=-1, keepdims=True) + RMS_EPS)
    return (y * g.astype(jnp.float32)).astype(x.dtype)


def _t5_bucket(rel):
    nb = N_BUCKETS // 2
    ret = (rel > 0).astype(jnp.int32) * nb
    n = jnp.abs(rel)
    max_exact = nb // 2
    nf = jnp.maximum(n, 1).astype(jnp.float32)
    large = max_exact + (jnp.log(nf / max_exact) / math.log(MAX_DISTANCE / max_exact)
                         * (nb - max_exact)).astype(jnp.int32)
    large = jnp.minimum(large, nb - 1)
    return ret + jnp.where(n < max_exact, n, large)


def _chunk_mask(q_pos, k_pos):
    return (k_pos // CHUNK)[None, :] <= (q_pos // CHUNK)[:, None]


def _rope_tables(seq):
    pos = jnp.arange(seq, dtype=jnp.float32)
    inv = ROPE_THETA ** (-jnp.arange(0, MLA_ROPE_DIM, 2, dtype=jnp.float32) / MLA_ROPE_DIM)
    ang = pos[:, None] * inv[None, :]
    return jnp.cos(ang), jnp.sin(ang)


def _apply_rope(x, cos, sin):
    xf = x.astype(jnp.float32)
    x1, x2 = jnp.split(xf, 2, axis=-1)
    out = jnp.concatenate([x1 * cos - x2 * sin, x1 * sin + x2 * cos], axis=-1)
    return out.astype(x.dtype)


def _diff_attention(q, k, v, lam, rel_bias):
    S = q.shape[1]
    scale = DIFF_HEAD_DIM ** -0.5
    outs = []
    for i in range(S // Q_BLOCK):
        L = (i + 1) * Q_BLOCK
        q_pos = jnp.arange(i * Q_BLOCK, L)
        k_pos = jnp.arange(L)
        q_b = q[:, i * Q_BLOCK:L]
        logits = jnp.einsum('bqhmd,bkhmd->bhmqk', q_b, k[:, :L]).astype(jnp.float32) * scale
        bias = rel_bias[_t5_bucket(k_pos[None, :] - q_pos[:, None])]
        bias = jnp.transpose(bias, (2, 0, 1)).astype(jnp.float32)
        logits = logits + bias[None, :, None]
        mask = _chunk_mask(q_pos, k_pos)
        logits = jnp.where(mask[None, None, None], logits, -jnp.inf)
        p = jax.nn.softmax(logits, axis=-1)
        a = p[:, :, 0] - lam * p[:, :, 1]
        outs.append(jnp.einsum('bhqk,bkhe->bqhe', a.astype(v.dtype), v[:, :L]))
    return jnp.concatenate(outs, axis=1)


def _mla_attention(q_nope, q_rope, k_nope, k_rope, v):
    S = q_nope.shape[1]
    scale = (MLA_NOPE_DIM + MLA_ROPE_DIM) ** -0.5
    outs = []
    for i in range(S // Q_BLOCK):
        L = (i + 1) * Q_BLOCK
        q_pos = jnp.arange(i * Q_BLOCK, L)
        k_pos = jnp.arange(L)
        qs = slice(i * Q_BLOCK, L)
        logits = (jnp.einsum('bqhd,bkhd->bhqk', q_nope[:, qs], k_nope[:, :L]).astype(jnp.float32)
                  + jnp.einsum('bqhr,bkr->bhqk', q_rope[:, qs], k_rope[:, :L]).astype(jnp.float32)) * scale
        mask = _chunk_mask(q_pos, k_pos)
        logits = jnp.where(mask[None, None], logits, -jnp.inf)
        p = jax.nn.softmax(logits, axis=-1)
        outs.append(jnp.einsum('bhqk,bkhe->bqhe', p.astype(v.dtype), v[:, :L]))
    return jnp.concatenate(outs, axis=1)


def setup_inputs(seed: int = 0) -> dict:
    key = jax.random.key(seed)
    ks = jax.random.split(key, 20)
    f32 = jnp.float32

    def nrm(k, shape, std):
        return jax.random.normal(k, shape, f32) * std

    x = jax.random.normal(ks[0], (BATCH, SEQ, D_MODEL), f32)
    w_in = nrm(ks[1], (DEPTH, D_MODEL, IN_WIDTH), D_MODEL ** -0.5)
    col_scale = jnp.ones((IN_WIDTH,), f32).at[2 * DIFF_WIDTH:3 * DIFF_WIDTH].set(BETA)
    w_in = w_in * col_scale
    lambda_q1 = nrm(ks[2], (DEPTH, DIFF_HEAD_DIM), 0.1)
    lambda_k1 = nrm(ks[3], (DEPTH, DIFF_HEAD_DIM), 0.1)
    lambda_q2 = nrm(ks[4], (DEPTH, DIFF_HEAD_DIM), 0.1)
    lambda_k2 = nrm(ks[5], (DEPTH, DIFF_HEAD_DIM), 0.1)
    subln_g = 1.0 + nrm(ks[6], (DEPTH, 2 * DIFF_HEAD_DIM), 0.02)
    q_norm_g = 1.0 + nrm(ks[7], (DEPTH, MLA_Q_RANK), 0.02)
    w_uq = nrm(ks[8], (DEPTH, MLA_Q_RANK, MLA_HEADS * (MLA_NOPE_DIM + MLA_ROPE_DIM)), MLA_Q_RANK ** -0.5)
    kv_norm_g = 1.0 + nrm(ks[9], (DEPTH, MLA_KV_RANK), 0.02)
    w_ukv = nrm(ks[10], (DEPTH, MLA_KV_RANK, MLA_HEADS, MLA_NOPE_DIM + MLA_V_DIM), MLA_KV_RANK ** -0.5)
    v_scale = jnp.concatenate([jnp.ones((MLA_NOPE_DIM,), f32), jnp.full((MLA_V_DIM,), BETA, f32)])
    w_ukv = (w_ukv * v_scale).reshape(DEPTH, MLA_KV_RANK, MLA_HEADS * (MLA_NOPE_DIM + MLA_V_DIM))
    rel_bias = nrm(ks[11], (N_BUCKETS, DIFF_HEADS), 0.5)
    w_o = nrm(ks[12], (DEPTH, MIX_WIDTH, D_MODEL), MIX_WIDTH ** -0.5 * BETA)
    ln1_g = 1.0 + nrm(ks[13], (DEPTH, D_MODEL), 0.02)
    ln1_b = nrm(ks[14], (DEPTH, D_MODEL), 0.02)
    w_mlp_in = nrm(ks[15], (DEPTH, D_MODEL, D_FF), D_MODEL ** -0.5 * BETA)
    w_mlp_out = nrm(ks[16], (DEPTH, D_FF, D_MODEL), D_FF ** -0.5 * BETA)
    ln2_g = 1.0 + nrm(ks[17], (DEPTH, D_MODEL), 0.02)
    ln2_b = nrm(ks[18], (DEPTH, D_MODEL), 0.02)
    return {"x": x, "w_in": w_in, "lambda_q1": lambda_q1, "lambda_k1": lambda_k1,
            "lambda_q2": lambda_q2, "lambda_k2": lambda_k2, "subln_g": subln_g,
            "q_norm_g": q_norm_g, "w_uq": w_uq, "kv_norm_g": kv_norm_g, "w_ukv": w_ukv,
            "rel_bias": rel_bias, "w_o": w_o, "ln1_g": ln1_g, "ln1_b": ln1_b,
            "w_mlp_in": w_mlp_in, "w_mlp_out": w_mlp_out, "ln2_g": ln2_g, "ln2_b": ln2_b}


def reference(x, w_in, lambda_q1, lambda_k1, lambda_q2, lambda_k2, subln_g,
              q_norm_g, w_uq, kv_norm_g, w_ukv, rel_bias, w_o, ln1_g, ln1_b,
              w_mlp_in, w_mlp_out, ln2_g, ln2_b):
    B, S, _ = x.shape
    cos, sin = _rope_tables(S)
    o_q = DIFF_WIDTH
    o_k = 2 * DIFF_WIDTH
    o_cq = 3 * DIFF_WIDTH
    o_ckv = o_cq + MLA_Q_RANK
    o_kr = o_ckv + MLA_KV_RANK
    for l in range(DEPTH):
        lambda_init = 0.8 - 0.6 * math.exp(-0.3 * l)
        h = jnp.einsum('bsd,de->bse', x, w_in[l])

        dq = h[..., :o_q].reshape(B, S, DIFF_HEADS, 2, DIFF_HEAD_DIM)
        dk = h[..., o_q:o_k].reshape(B, S, DIFF_HEADS, 2, DIFF_HEAD_DIM)
        dv = h[..., o_k:o_cq].reshape(B, S, DIFF_HEADS, 2 * DIFF_HEAD_DIM)
        lam = (jnp.exp(jnp.sum(lambda_q1[l].astype(jnp.float32) * lambda_k1[l].astype(jnp.float32)))
               - jnp.exp(jnp.sum(lambda_q2[l].astype(jnp.float32) * lambda_k2[l].astype(jnp.float32)))
               + lambda_init)
        a_out = _diff_attention(dq, dk, dv, lam, rel_bias)
        a_out = _rms_norm(a_out, subln_g[l]) * (1.0 - lambda_init)

        c_q = _rms_norm(h[..., o_cq:o_ckv], q_norm_g[l])
        c_kv = _rms_norm(h[..., o_ckv:o_kr], kv_norm_g[l])
        k_rope = _apply_rope(h[..., o_kr:], cos, sin)
        qf = jnp.einsum('bsr,re->bse', c_q, w_uq[l]).reshape(B, S, MLA_HEADS, MLA_NOPE_DIM + MLA_ROPE_DIM)
        q_nope = qf[..., :MLA_NOPE_DIM]
        q_rope = _apply_rope(qf[..., MLA_NOPE_DIM:], cos[:, None, :], sin[:, None, :])
        kvf = jnp.einsum('bsr,re->bse', c_kv, w_ukv[l]).reshape(B, S, MLA_HEADS, MLA_NOPE_DIM + MLA_V_DIM)
        k_nope = kvf[..., :MLA_NOPE_DIM]
        mv = kvf[..., MLA_NOPE_DIM:]
        b_out = _mla_attention(q_nope, q_rope, k_nope, k_rope, mv)

        mix = jnp.concatenate([a_out.reshape(B, S, DIFF_WIDTH), b_out.reshape(B, S, MLA_WIDTH)], axis=-1)
        y = jnp.einsum('bse,ed->bsd', mix, w_o[l])
        x = _layer_norm(ALPHA * x + y, ln1_g[l], ln1_b[l])

        u = jnp.square(jax.nn.relu(jnp.einsum('bsd,df->bsf', x, w_mlp_in[l])))
        y = jnp.einsum('bsf,fd->bsd', u, w_mlp_out[l])
        x = _layer_norm(ALPHA * x + y, ln2_g[l], ln2_b[l])
    return x
```

```python
import math
import os
from contextlib import ExitStack

import numpy as np
import concourse.bass as bass
import concourse.mybir as mybir
from concourse.bass_utils import run_bass_kernel_spmd

F32 = mybir.dt.float32
BF16 = mybir.dt.bfloat16
AF = mybir.ActivationFunctionType
ALU = mybir.AluOpType

D = 1024
S = 2048
DEPTH = 2
NT_ = S // 128
IN_W = 1984
DFF = 4096
ALPHA = (2 * DEPTH) ** 0.25
LN_EPS = 1e-5
RMS_EPS = 1e-6
ROPE_DB = True
NEG = -30000.0
DIFF_SCALE = 64 ** -0.5
MLA_SCALE = 192 ** -0.5


class Sched:
    ENG = ('pe', 'act', 'dve', 'pool', 'sp')

    def __init__(self):
        self.ops = {e: [] for e in self.ENG}
        self.cnt = {}
        self.known = {e: {} for e in self.ENG}
        self.snap = {}
        self.res = {}

    def _learn(self, eng, s, v):
        kn = self.known[eng]
        if kn.get(s, 0) < v:
            kn[s] = v
        sn = self.snap.get((s, v))
        if sn:
            for s2, v2 in sn.items():
                if kn.get(s2, 0) < v2:
                    kn[s2] = v2

    def _deps(self, eng, reads, writes, after=()):
        deps = {}

        def add(s, v):
            if deps.get(s, 0) < v:
                deps[s] = v
        for r in reads:
            st = self.res.get(r)
            if st:
                for s, v in st[0].items():
                    if eng == 'pe' and s == 'pe':
                        continue
                    add(s, v)
        for w in writes:
            st = self.res.get(w)
            if st:
                for s, v in st[0].items():
                    if eng == 'pe' and s == 'pe':
                        continue
                    add(s, v)
                for s, v in st[1].items():
                    if s == eng and eng == 'pe':
                        continue
                    add(s, v)
        for w in after:
            st = self.res.get(w)
            if st:
                for s, v in st[1].items():
                    if s == eng and eng == 'pe':
                        continue
                    add(s, v)
        kn = self.known[eng]
        waits = [(s, v) for s, v in deps.items() if kn.get(s, 0) < v]
        for s, v in waits:
            self._learn(eng, s, v)
        return waits

    def _mark(self, ev, reads, writes):
        s, v = ev
        for r in reads:
            self.res.setdefault(r, ({}, {}))[1][s] = v
        for w in writes:
            self.res.setdefault(w, ({}, {}))[0][s] = v

    def op(self, eng, fn, reads=(), writes=(), after=()):
        waits = self._deps(eng, reads, writes, after)
        self.cnt[eng] = self.cnt.get(eng, 0) + 1
        ev = (eng, self.cnt[eng])
        sn = dict(self.known[eng])
        sn[eng] = ev[1]
        self.snap[ev] = sn
        self.ops[eng].append((waits, fn, (eng, 1)))
        self._mark(ev, reads, writes)

    def dma(self, q, semkey, fn, reads=(), writes=()):
        waits = self._deps(q, reads, writes)
        self.cnt[semkey] = self.cnt.get(semkey, 0) + 16
        ev = (semkey, self.cnt[semkey])
        self.snap[ev] = dict(self.known[q])
        self.ops[q].append((waits, fn, (semkey, 16)))
        self._mark(ev, reads, writes)

    def replay(self, name, eng, sems):
        for waits, fn, inc in self.ops[name]:
            for s, v in waits:
                eng.wait_ge(sems[s], v)
            r = fn(eng)
            r.then_inc(sems[inc[0]], inc[1])


def build_program(debug_stop=None):
    STOP = debug_stop
    nc = bass.Bass("TRN2", target_bir_lowering=False)
    SC = Sched()

    def din(name, shape):
        return nc.dram_tensor(name, list(shape), F32, kind="ExternalInput").ap()

    x_d = din("x", (S, D))
    w_in_d = din("w_in", (DEPTH, D, IN_W))
    krsw_d = din("w_in_krsw", (DEPTH, D, 64))
    lq1_d = din("lambda_q1", (DEPTH, 64))
    lk1_d = din("lambda_k1", (DEPTH, 64))
    lq2_d = din("lambda_q2", (DEPTH, 64))
    lk2_d = din("lambda_k2", (DEPTH, 64))
    subg_d = din("subln_g", (DEPTH, 128))
    qg_d = din("q_norm_g", (DEPTH, 256))
    wuq_d = din("w_uq", (DEPTH, 256, 768))
    wuqsw_d = din("w_uq_sw", (DEPTH, 256, 256))
    kvg_d = din("kv_norm_g", (DEPTH, 128))
    wukv_d = din("w_ukv", (DEPTH, 128, 1024))
    relb_d = din("rel_bias", (32, 4))
    bt_d = din("bias_tiles", (4, 128, 256))
    mask_d = din("mask_tile", (128, 128))
    wo_d = din("w_o", (DEPTH, D, D))
    ln1g_d = din("ln1_g", (DEPTH, D))
    ln1b_d = din("ln1_b", (DEPTH, D))
    w1_d = din("w_mlp_in", (DEPTH, D, DFF))
    w2_d = din("w_mlp_out", (DEPTH, DFF, D))
    ln2g_d = din("ln2_g", (DEPTH, D))
    ln2b_d = din("ln2_b", (DEPTH, D))
    cos_d = din("cos2T", (64, S))
    sin_d = din("sin2sT", (64, S))
    ident_d = din("ident", (128, 128))
    out_d = nc.dram_tensor("out", [S, D], F32, kind="ExternalOutput").ap()

    es = ExitStack()

    def sb(name, shape, dt):
        return es.enter_context(nc.sbuf_tensor(name, list(shape), dt))

    with es:
        x_tm = sb("x_tm", (128, NT_, D), F32)
        lnG = sb("lnG", (128, D), F32)
        lnB = sb("lnB", (128, D), F32)
        Rb = [sb("R0", (128, 512), F32), sb("R1", (128, 512), F32)]
        Tb = [sb("T0", (128, 512), F32), sb("T1", (128, 512), F32)]
        cosb2 = [sb("cosb_a", (64, 512), F32), sb("cosb_b", (64, 512), F32)]
        sinb2 = [sb("sinb_a", (64, 512), F32), sb("sinb_b", (64, 512), F32)]
        ident = sb("ident32", (128, 128), F32)
        ones32 = sb("ones32", (128, 128), F32)
        small = sb("small", (128, 64), F32)
        lamt = sb("lamt", (128, 4, 64), F32)
        bnst = sb("bnst", (128, 48), F32)

        xT = sb("xT", (128, 8, S), BF16)
        VU = sb("VU", (128, 8192), BF16)
        Qs = sb("Qs", (128, S), BF16)
        Ks = sb("Ks", (128, S), BF16)
        QR = sb("QR", (128, S), BF16)
        CW = sb("CW", (128, 4096), BF16)
        ckvT = sb("ckvT", (128, S), BF16)
        krT = sb("krT", (128, S), BF16)
        mixT = sb("mixT", (128, S), BF16)
        Wqk = sb("Wqk", (128, 8, 256), BF16)
        Wbig = sb("Wbig", (128, 8, 512), BF16)
        Wuq = sb("Wuq", (128, 2, 1024), BF16)
        Wkn = sb("Wkn", (128, 512), BF16)
        Wmv = sb("Wmv", (128, 512), BF16)
        Wo2 = [sb("Wo_a", (128, D), BF16), sb("Wo_b", (128, D), BF16), sb("Wo_c", (128, D), BF16)]
        Eb = [sb("E%d" % i, (128, 512), BF16) for i in range(4)]
        ones16 = sb("ones16", (128, 128), BF16)
        ident16 = sb("ident16", (128, 128), BF16)
        bt_hi = sb("bt_hi", (128, 4, 256), BF16)
        bt_lo = sb("bt_lo", (128, 4, 256), BF16)
        mk16 = sb("mk16", (128, 128), BF16)
        PS = [es.enter_context(nc.psum_tensor("ps%d" % i, [128, 512], F32)) for i in range(8)]

        cqT = CW[:, :].rearrange("p (j n) -> p j n", j=2)
        W2 = CW[:, :].rearrange("p (j n) -> p j n", j=4)

        C_EPSR, C_EPSL, C_NLAM, C_GS, C_GKV, C_GQ0, C_GQ1, C_CB = 0, 1, 2, 3, 4, 5, 6, 8
        C_E1, C_E2, C_MV, C_RSTD, C_NMR = 12, 13, 20, 22, 23

        def col(c):
            return small[:, c:c + 1]

        def wload(dst_ap, src_ap, wkey, semkey):
            SC.dma('pool', semkey, lambda e: e.dma_start(out=dst_ap, in_=src_ap), writes=[wkey])

        ps_rr = [0]

        def ps_any():
            b = ps_rr[0] % 8
            ps_rr[0] += 1
            return b

        misc_rr = [0]

        def ps_misc():
            b = 7
            misc_rr[0] += 1
            return b

        ev_rr = [0]

        def evac(out_ap, in_ap, reads, writes):
            k = ev_rr[0] % 2
            ev_rr[0] += 1
            if k == 0:
                SC.op('act', lambda e: e.copy(out=out_ap, in_=in_ap), reads=reads, writes=writes)
            else:
                SC.op('dve', lambda e: e.tensor_copy(out=out_ap, in_=in_ap), reads=reads, writes=writes)

        cdma_i = [0]

        def const_dma(out_ap, in_ap, wkey):
            cdma_i[0] += 1
            SC.dma('sp', 'c%d' % cdma_i[0], lambda e: e.dma_start(out=out_ap, in_=in_ap), writes=[wkey])

        const_dma(ident[:, :], ident_d[:, :], ('ident',))
        ntile = lnG[:, 0:1024].rearrange("p (h n) -> p h n", h=4)
        maskt = lnB[:, 0:128]
        const_dma(maskt, mask_d[:, :], ('lnB',))
        const_dma(ntile, bt_d.rearrange("h p n -> p h n"), ('lnG',))
        SC.op('pool', lambda e: e.memset(ones32[:, :], 1.0), writes=[('ones32',)])
        SC.op('pool', lambda e: e.memset(ones16[:, :], 1.0), writes=[('ones16',)])
        SC.op('pool', lambda e: e.memset(QR[64:128, :], 0.0), writes=[('QR', tb) for tb in range(4)])
        SC.op('pool', lambda e: e.memset(krT[64:128, :], 0.0), writes=[('kr', tb) for tb in range(4)])
        SC.op('pool', lambda e: e.memset(col(C_EPSR), RMS_EPS), writes=[('sm', C_EPSR)])
        SC.op('pool', lambda e: e.memset(col(C_EPSL), LN_EPS), writes=[('sm', C_EPSL)])
        for h in range(4):
            const_dma(col(C_CB + h), relb_d[15, h:h + 1].partition_broadcast(128), ('sm', C_CB + h))

        SC.op('dve', lambda e: e.tensor_copy(out=ident16[:, :], in_=ident[:, :]), reads=[('ident',)], writes=[('ident16',)])
        SC.op('dve', lambda e: e.tensor_scalar(out=mk16[:, :], in0=maskt, scalar1=1.0 / MLA_SCALE, scalar2=None,
                                               op0=ALU.mult), reads=[('lnB',)], writes=[('bt16',)])
        for h in range(4):
            SC.op('dve', lambda e, h=h: e.tensor_scalar(out=ntile[:, h, :], in0=ntile[:, h, :], scalar1=col(C_CB + h),
                                                        scalar2=None, op0=ALU.subtract),
                  reads=[('lnG',), ('sm', C_CB + h)], writes=[('lnG',)])
            SC.op('dve', lambda e, h=h: e.tensor_scalar(out=ntile[:, h, :], in0=ntile[:, h, :], scalar1=1.0 / DIFF_SCALE,
                                                        scalar2=None, op0=ALU.mult),
                  reads=[('lnG',)], writes=[('lnG',)])
            SC.op('dve', lambda e, h=h: e.tensor_copy(out=bt_hi[:, h, :], in_=ntile[:, h, :]),
                  reads=[('lnG',)], writes=[('bt16',)])
            SC.op('dve', lambda e, h=h: e.tensor_tensor(out=bt_lo[:, h, :], in0=ntile[:, h, :], in1=bt_hi[:, h, :],
                                                        op=ALU.subtract),
                  reads=[('lnG',), ('bt16',)], writes=[('bt16',)])

        for t in range(NT_):
            SC.dma('sp', 'xl%d' % t,
                   lambda e, t=t: e.dma_start(out=x_tm[:, t, :], in_=x_d[t * 128:(t + 1) * 128, :]),
                   writes=[('x', t, 0), ('x', t, 1)])

        def make_xT(t, prescaled=False, inflight=False):
            for g in range(2):
                b = ps_misc() if inflight else ps_any()

                def f(e, t=t, g=g, b=b):
                    r = None
                    for j in range(4):
                        c = g * 4 + j
                        r = e.transpose(PS[b][:, j * 128:(j + 1) * 128], x_tm[:, t, c * 128:(c + 1) * 128], ident[:, :])
                    return r
                SC.op('pe', f, reads=[('x', t, g), ('ident',)], writes=[('ps', b)])
                if prescaled and inflight:
                    SC.op('dve', lambda e, t=t, g=g, b=b: e.tensor_scalar(
                        out=xT[:, g * 4:(g + 1) * 4, t * 128:(t + 1) * 128],
                        in0=PS[b][:, :].rearrange("p (j n) -> p j n", j=4), scalar1=1.0 / ALPHA, scalar2=None,
                        op0=ALU.mult),
                        reads=[('ps', b)], writes=[('xT', t // 4)])
                elif prescaled:
                    SC.op('act', lambda e, t=t, g=g, b=b: e.mul(
                        out=xT[:, g * 4:(g + 1) * 4, t * 128:(t + 1) * 128],
                        in_=PS[b][:, :].rearrange("p (j n) -> p j n", j=4), mul=1.0 / ALPHA),
                        reads=[('ps', b)], writes=[('xT', t // 4)])
                else:
                    SC.op('dve', lambda e, t=t, g=g, b=b: e.tensor_copy(
                        out=xT[:, g * 4:(g + 1) * 4, t * 128:(t + 1) * 128],
                        in_=PS[b][:, :].rearrange("p (j n) -> p j n", j=4)),
                        reads=[('ps', b)], writes=[('xT', t // 4)])
            if not prescaled:
                SC.op('act', lambda e, t=t: e.mul(out=x_tm[:, t, :], in_=x_tm[:, t, :], mul=ALPHA),
                      reads=[('x', t, 0), ('x', t, 1)], writes=[('x', t, 0), ('x', t, 1)])

        for t in range(NT_):
            make_xT(t)

        def rope_block(ba, bb, tb, out_ap, wkey):
            k = (tb % 2) if ROPE_DB else 0
            cosb, sinb = cosb2[k], sinb2[k]
            (t0, t0k), (t1, t1k) = ((Tb[0], ('T', 0)), (Tb[1], ('T', 1)))
            SC.dma('sp', 'ropec%d' % k, lambda e: e.dma_start(out=cosb[:, :], in_=cos_d[:, tb * 512:(tb + 1) * 512]),
                   writes=[('cosb', k)])
            SC.dma('sp', 'ropes%d' % k, lambda e: e.dma_start(out=sinb[:, :], in_=sin_d[:, tb * 512:(tb + 1) * 512]),
                   writes=[('sinb', k)])
            SC.op('dve', lambda e: e.tensor_tensor(out=t0[0:64, :], in0=PS[ba][0:64, :], in1=cosb[:, :], op=ALU.mult),
                  reads=[('ps', ba), ('cosb', k)], writes=[t0k])
            SC.op('dve', lambda e: e.tensor_tensor(out=t1[0:64, :], in0=PS[bb][0:64, :], in1=sinb[:, :], op=ALU.mult),
                  reads=[('ps', bb), ('sinb', k)], writes=[t1k])
            SC.op('pool', lambda e: e.tensor_tensor(out=out_ap, in0=t0[0:64, :], in1=t1[0:64, :], op=ALU.add),
                  reads=[t0k, t1k], writes=[wkey])

        def wo_accum(sp, wi):
            for tt in range(4):
                t = sp * 4 + tt
                for hf in range(2):
                    mb = ps_misc()
                    SC.op('pe', lambda e, t=t, hf=hf, mb=mb: e.matmul(
                        PS[mb][:, :], lhsT=mixT[:, t * 128:(t + 1) * 128], rhs=Wo2[wi][:, hf * 512:(hf + 1) * 512],
                        start=True, stop=True),
                        reads=[('mix', sp), ('Wo', wi)], writes=[('ps', mb)])
                    SC.op('dve', lambda e, t=t, hf=hf, mb=mb: e.tensor_tensor(
                        out=x_tm[:, t, hf * 512:(hf + 1) * 512], in0=PS[mb][:, :],
                        in1=x_tm[:, t, hf * 512:(hf + 1) * 512], op=ALU.add),
                        reads=[('ps', mb), ('x', t, hf)], writes=[('x', t, hf)])
                    yield 'work'

        D1, D2 = 4, 5

        def half_post(k):
            SC.op('act', lambda e: e.activation(out=Rb[k][:, :], in_=PS[5 + k][:, :], func=AF.Ln),
                  reads=[('ps', 5 + k)], writes=[('R', k)])
            SC.op('act', lambda e: e.activation(out=Rb[k][:, :], in_=Rb[k][:, :], func=AF.Exp, scale=-1.0),
                  reads=[('R', k)], writes=[('R', k)])
            SC.op('dve', lambda e: e.tensor_tensor(out=Tb[k][:, :], in0=PS[3 + k][:, :], in1=Rb[k][:, :], op=ALU.mult),
                  reads=[('ps', 3 + k), ('R', k)], writes=[('T', k)])

        def post_diff(sp, wi, state):
            SC.op('dve', lambda e: e.scalar_tensor_tensor(out=Tb[0][:, :], in0=Tb[1][:, :], scalar=col(C_NLAM),
                                                          in1=Tb[0][:, :], op0=ALU.mult, op1=ALU.add),
                  reads=[('T', 0), ('T', 1), ('sm', C_NLAM)], writes=[('T', 0)])
            SC.op('pool', lambda e: e.tensor_tensor(out=Rb[0][:, :], in0=Tb[0][:, :], in1=Tb[0][:, :], op=ALU.mult),
                  reads=[('T', 0)], writes=[('R', 0)])
            for _ in range(D1):
                yield 'delay'
            mb = ps_misc()
            SC.op('pe', lambda e: e.matmul(PS[mb][:, :], lhsT=ones32[:, :], rhs=Rb[0][:, :], start=True, stop=True),
                  reads=[('R', 0), ('ones32',)], writes=[('ps', mb)])
            SC.op('act', lambda e: e.activation(out=Rb[1][:, :], in_=PS[mb][:, :], func=AF.Ln,
                                                bias=col(C_EPSR), scale=1.0 / 128.0),
                  reads=[('ps', mb), ('sm', C_EPSR)], writes=[('R', 1)])
            SC.op('act', lambda e: e.activation(out=Rb[1][:, :], in_=Rb[1][:, :], func=AF.Exp, scale=-0.5),
                  reads=[('R', 1)], writes=[('R', 1)])
            SC.op('dve', lambda e: e.scalar_tensor_tensor(out=mixT[:, sp * 512:(sp + 1) * 512], in0=Tb[0][:, :],
                                                          scalar=col(C_GS), in1=Rb[1][:, :],
                                                          op0=ALU.mult, op1=ALU.mult),
                  reads=[('T', 0), ('R', 1), ('sm', C_GS)], writes=[('mix', sp)])
            state['released'] = True
            for _ in range(D2):
                yield 'delay'
            yield from wo_accum(sp, wi)

        def post_mla(sp, wi, state):
            ob, smb = 3 + sp % 2, 5 + sp % 2
            k = sp % 2
            SC.op('act', lambda e: e.activation(out=Rb[k][:, :], in_=PS[smb][:, :], func=AF.Ln),
                  reads=[('ps', smb)], writes=[('R', k)])
            SC.op('act', lambda e: e.activation(out=Rb[k][:, :], in_=Rb[k][:, :], func=AF.Exp, scale=-1.0),
                  reads=[('R', k)], writes=[('R', k)])
            SC.op('dve', lambda e: e.tensor_tensor(out=mixT[:, sp * 512:(sp + 1) * 512], in0=PS[ob][:, :],
                                                   in1=Rb[k][:, :], op=ALU.mult),
                  reads=[('ps', ob), ('R', k)], writes=[('mix', sp)])
            state['released'] = True
            for _ in range(D2):
                yield 'delay'
            yield from wo_accum(sp, wi)

        PEND = []

        ON_SPAN_DONE = [None]

        def pend_advance(p):
            try:
                p['last'] = next(p['gen'])
                return True
            except StopIteration:
                PEND.remove(p)
                if p.get('hook'):
                    p['hook'](p['sp'])
                return False

        def pend_tick():
            first = True
            for p in list(PEND):
                if first or p['last'] == 'delay':
                    pend_advance(p)
                first = False

        def pend_drain(pred=lambda p: True, until_released=False):
            for p in list(PEND):
                if pred(p):
                    while p in PEND and not (until_released and p['state']['released']):
                        pend_advance(p)

        def pend_add(kind, sp, wi):
            pend_drain(lambda p: not p['state']['released'], until_released=True)
            pend_drain(lambda p: p['sp'] == sp)
            state = {'released': False}
            gen = post_diff(sp, wi, state) if kind == 'diff' else post_mla(sp, wi, state)
            p = {'gen': gen, 'sp': sp, 'wi': wi, 'state': state, 'last': None, 'hook': ON_SPAN_DONE[0]}
            PEND.append(p)
            pend_advance(p)

        def attention(kind, h, wi):
            diff = (kind == 'diff')
            nmaps = 2 if diff else 1
            scale = DIFF_SCALE if diff else MLA_SCALE
            steps = []
            for sp in range(4):
                for m in range(nmaps):
                    for kb in range(4 * sp + 4):
                        steps.append((sp, m, kb))

            def geom(st):
                sp, m, kb = st
                qb0 = 4 * sp
                first = max(kb, qb0)
                off = (first - qb0) * 128
                return qb0, first, off, 512 - off

            def emit_qk(i):
                sp, m, kb = steps[i]
                qb0, first, off, N = geom(steps[i])
                sbk = i % 3
                q0 = first * 128
                if diff:
                    if kb >= qb0:
                        nw, c0 = min(256, N), 0
                    elif kb == qb0 - 1:
                        nw, c0 = 128, 128
                    else:
                        nw, c0 = 0, 0
                else:
                    nw, c0 = (128, 0) if kb >= qb0 else (0, 0)

                def f(e):
                    if diff:
                        qm = Qs if m == 0 else ckvT
                        r = e.matmul(PS[sbk][:, 0:N], lhsT=Ks[:, kb * 128:(kb + 1) * 128],
                                     rhs=qm[:, q0:q0 + N], start=True, stop=(nw == 0))
                        if nw:
                            e.matmul(PS[sbk][:, 0:nw], lhsT=ident16[:, :], rhs=bt_hi[:, h, c0:c0 + nw], start=False, stop=False)
                            r = e.matmul(PS[sbk][:, 0:nw], lhsT=ident16[:, :], rhs=bt_lo[:, h, c0:c0 + nw], start=False, stop=True)
                    else:
                        e.matmul(PS[sbk][:, 0:N], lhsT=Ks[:, kb * 128:(kb + 1) * 128], rhs=Qs[:, q0:q0 + N],
                                 start=True, stop=False)
                        r = e.matmul(PS[sbk][:, 0:N], lhsT=krT[:, kb * 128:(kb + 1) * 128],
                                     rhs=QR[:, q0:q0 + N], start=False, stop=(nw == 0))
                        if nw:
                            r = e.matmul(PS[sbk][:, 0:nw], lhsT=ident16[:, :], rhs=mk16[:, 0:nw], start=False, stop=True)
                    return r
                rd = [('K', kb // 4), ('Q', sp) if (m == 0 or not diff) else ('ckv', sp)]
                if not diff:
                    rd += [('kr', kb // 4), ('QR', sp)]
                if nw:
                    rd += [('bt16',), ('ident16',)]
                SC.op('pe', f, reads=rd, writes=[('ps', sbk)])

            def emit_exp_av(i):
                sp, m, kb = steps[i]
                qb0, first, off, N = geom(steps[i])
                sbk, ei = i % 3, i % 4
                E = Eb[ei]
                if diff:
                    SC.op('act', lambda e: e.activation(out=E[:, 0:N], in_=PS[sbk][:, 0:N], func=AF.Exp, scale=scale),
                          reads=[('ps', sbk)], writes=[('E', ei)])
                    ob, smb = 3 + m, 5 + m
                else:
                    SC.op('act', lambda e: e.activation(out=E[:, 0:N], in_=PS[sbk][:, 0:N], func=AF.Exp, scale=scale),
                          reads=[('ps', sbk)], writes=[('E', ei)])
                    ob, smb = 3 + sp % 2, 5 + sp % 2
                last = 4 * sp + 3
                vcol = kb * 512 + h * 128

                def f(e):
                    e.matmul(PS[ob][:, off:512], lhsT=VU[:, vcol:vcol + 128], rhs=E[:, 0:N],
                             start=(kb == 0), stop=(kb == last))
                    return e.matmul(PS[smb][:, off:512], lhsT=ones16[:, :], rhs=E[:, 0:N],
                                    start=(kb == 0), stop=(kb == last))
                SC.op('pe', f, reads=[('VU', kb), ('ones16',), ('E', ei)], writes=[('ps', ob), ('ps', smb)])

            LA = 2
            deferred = []
            for i in range(min(LA, len(steps))):
                emit_qk(i)
            for i, st in enumerate(steps):
                if i + LA < len(steps):
                    emit_qk(i + LA)
                emit_exp_av(i)
                sp, m, kb = st
                for d in deferred:
                    d[0] -= 1
                while deferred and deferred[0][0] <= 0:
                    deferred.pop(0)[1]()
                if diff and kb == 4 * sp + 3:
                    def hp(m=m, sp=sp):
                        if m == 0:
                            pend_drain(lambda p: not p['state']['released'], until_released=True)
                        half_post(m)
                        if m == 1:
                            pend_add(kind, sp, wi)
                    deferred.append([2, hp])
                    pend_tick()
                elif (not diff) and kb == 4 * sp + 3:
                    pend_add(kind, sp, wi)
                else:
                    pend_tick()
            while deferred:
                deferred.pop(0)[1]()

        class LNPipe:
            def __init__(self, g_d, b_d, l, last):
                self.last = last
                self.q = []
                self.inflight = False
                SC.dma('sp', 'lng', lambda e: e.dma_start(out=lnG[:, :], in_=g_d[l, :].partition_broadcast(128)),
                       writes=[('lnG',)])
                SC.dma('sp', 'lnb', lambda e: e.dma_start(out=lnB[:, :], in_=b_d[l, :].partition_broadcast(128)),
                       writes=[('lnB',)])
                if not last:
                    SC.op('pool', lambda e: e.tensor_scalar(out=lnG[:, :], in0=lnG[:, :], scalar1=ALPHA, scalar2=1.0,
                                                            op0=ALU.mult, op1=ALU.mult),
                          reads=[('lnG',)], writes=[('lnG',)])
                    SC.op('pool', lambda e: e.tensor_scalar(out=lnB[:, :], in0=lnB[:, :], scalar1=ALPHA, scalar2=1.0,
                                                            op0=ALU.mult, op1=ALU.mult),
                          reads=[('lnB',)], writes=[('lnB',)])

            def stage_a1(self, t):
                xr = [('x', t, 0), ('x', t, 1)]
                k = t % 4
                cm, cr, cn = C_MV + 4 * k, C_RSTD + 4 * k, C_NMR + 4 * k
                SC.op('dve', lambda e: e.bn_stats(out=bnst[:, k * 12:k * 12 + 6], in_=x_tm[:, t, 0:512]),
                      reads=xr, writes=[('bn', k, 0)])
                SC.op('dve', lambda e: e.bn_stats(out=bnst[:, k * 12 + 6:k * 12 + 12], in_=x_tm[:, t, 512:1024]),
                      reads=xr, writes=[('bn', k, 1)])
                SC.op('dve', lambda e: e.bn_aggr(out=small[:, cm:cm + 2], in_=bnst[:, k * 12:k * 12 + 12]),
                      reads=[('bn', k, 0), ('bn', k, 1)], writes=[('sm', cm)])
                if self.inflight:
                    SC.op('act', lambda e: e.activation(out=col(cr), in_=small[:, cm + 1:cm + 2], func=AF.Ln,
                                                        bias=col(C_EPSL), scale=1.0),
                          reads=[('sm', cm), ('sm', C_EPSL)], writes=[('sm', cr)])
                    SC.op('act', lambda e: e.activation(out=col(cr), in_=col(cr), func=AF.Exp, scale=-0.5),
                          reads=[('sm', cr)], writes=[('sm', cr)])
                else:
                    SC.op('act', lambda e: e.activation(out=col(cr), in_=small[:, cm + 1:cm + 2], func=AF.Sqrt,
                                                        bias=col(C_EPSL), scale=1.0),
                          reads=[('sm', cm), ('sm', C_EPSL)], writes=[('sm', cr)])
                    SC.op('dve', lambda e: e.reciprocal(out=col(cr), in_=col(cr)), reads=[('sm', cr)], writes=[('sm', cr)])
                if not self.inflight:
                    SC.op('dve', lambda e: e.scalar_tensor_tensor(
                        out=col(cn), in0=small[:, cm:cm + 1], scalar=-1.0, in1=col(cr), op0=ALU.mult, op1=ALU.mult),
                        reads=[('sm', cm), ('sm', cr)], writes=[('sm', cn)])

            def stage_a2(self, t):
                xr = [('x', t, 0), ('x', t, 1)]
                k = t % 4
                cm, cr, cn = C_MV + 4 * k, C_RSTD + 4 * k, C_NMR + 4 * k
                if self.inflight:
                    SC.op('dve', lambda e: e.tensor_scalar(out=x_tm[:, t, :], in0=x_tm[:, t, :],
                                                           scalar1=small[:, cm:cm + 1], scalar2=col(cr),
                                                           op0=ALU.subtract, op1=ALU.mult),
                          reads=xr + [('sm', cm), ('sm', cr)], writes=xr)
                else:
                    SC.op('act', lambda e: e.activation(out=x_tm[:, t, :], in_=x_tm[:, t, :],
                                                        func=AF.Identity, bias=col(cn), scale=col(cr)),
                          reads=xr + [('sm', cr), ('sm', cn)], writes=xr)

            def stage_a3(self, t):
                xr = [('x', t, 0), ('x', t, 1)]
                SC.op('pool', lambda e: e.tensor_tensor(out=x_tm[:, t, :], in0=x_tm[:, t, :], in1=lnG[:, :], op=ALU.mult),
                      reads=xr + [('lnG',)], writes=xr)

            def stage_b(self, t):
                xr = [('x', t, 0), ('x', t, 1)]
                SC.op('dve', lambda e: e.tensor_tensor(out=x_tm[:, t, :], in0=x_tm[:, t, :], in1=lnB[:, :], op=ALU.add),
                      reads=xr + [('lnB',)], writes=xr)

            def stage_c(self, t):
                if self.last:
                    SC.dma('sp', 'out', lambda e: e.dma_start(out=out_d[t * 128:(t + 1) * 128, :], in_=x_tm[:, t, :]),
                           reads=[('x', t, 0), ('x', t, 1)])
                else:
                    make_xT(t, prescaled=True, inflight=self.inflight)

            def push(self, t):
                self.q.append(t)
                self._run(len(self.q) - 1)

            def _run(self, n):
                for s, fn in enumerate((self.stage_a1, self.stage_a2, self.stage_a3, self.stage_b, self.stage_c)):
                    i = n - s
                    if 0 <= i < len(self.q):
                        fn(self.q[i])

            def flush(self):
                n = len(self.q)
                for extra in range(4):
                    self._run(n + extra)

        def layer_norm(g_d, b_d, l, last):
            ln = LNPipe(g_d, b_d, l, last)
            for t in range(NT_):
                ln.push(t)
            ln.flush()

        def xT_blk(c, tb):
            return xT[:, c, tb * 512:(tb + 1) * 512]

        def proj_fm(lhs_fn, rhs_fn, nk, out_buf, okey, rkeys):
            for tb in range(4):
                b = ps_any()

                def f(e, tb=tb, b=b):
                    r = None
                    for c in range(nk):
                        r = e.matmul(PS[b][:, :], lhsT=lhs_fn(c), rhs=rhs_fn(c, tb), start=(c == 0), stop=(c == nk - 1))
                    return r
                SC.op('pe', f, reads=[rk(tb) if callable(rk) else rk for rk in rkeys], writes=[('ps', b)])
                evac(out_buf[:, tb * 512:(tb + 1) * 512], PS[b][:, :], [('ps', b)], [(okey, tb)])

        def load_wbig(src_fn, tag):
            for g in range(2):
                for (dst, s) in src_fn(g):
                    wload(dst, s, ('Wbig', g), 'wbig%d' % g)

        def wb4(g):
            return Wbig[:, g * 4:(g + 1) * 4, :]

        def load_wo(l, row0, wi):
            pend_drain(lambda p: p['wi'] == wi)
            wload(Wo2[wi][:, :], wo_d[l, row0:row0 + 128, :], ('Wo', wi), 'wo%d' % wi)

        def layer_consts(l):
            lambda_init = 0.8 - 0.6 * math.exp(-0.3 * l)
            for i, dsrc in enumerate((lq1_d, lk1_d, lq2_d, lk2_d)):
                const_dma(lamt[:, i, :], dsrc[l, :].partition_broadcast(128), ('lamt', i))
            SC.op('dve', lambda e: e.tensor_tensor(out=lamt[:, 0, :], in0=lamt[:, 0, :], in1=lamt[:, 1, :], op=ALU.mult),
                  reads=[('lamt', 0), ('lamt', 1)], writes=[('lamt', 0)])
            SC.op('dve', lambda e: e.tensor_tensor(out=lamt[:, 2, :], in0=lamt[:, 2, :], in1=lamt[:, 3, :], op=ALU.mult),
                  reads=[('lamt', 2), ('lamt', 3)], writes=[('lamt', 2)])
            SC.op('dve', lambda e: e.tensor_reduce(out=col(C_E1), in_=lamt[:, 0, :], axis=mybir.AxisListType.X, op=ALU.add),
                  reads=[('lamt', 0)], writes=[('sm', C_E1)])
            SC.op('dve', lambda e: e.tensor_reduce(out=col(C_E2), in_=lamt[:, 2, :], axis=mybir.AxisListType.X, op=ALU.add),
                  reads=[('lamt', 2)], writes=[('sm', C_E2)])
            SC.op('act', lambda e: e.activation(out=col(C_E1), in_=col(C_E1), func=AF.Exp),
                  reads=[('sm', C_E1)], writes=[('sm', C_E1)])
            SC.op('act', lambda e: e.activation(out=col(C_E2), in_=col(C_E2), func=AF.Exp),
                  reads=[('sm', C_E2)], writes=[('sm', C_E2)])
            SC.op('dve', lambda e: e.tensor_tensor(out=col(C_NLAM), in0=col(C_E2), in1=col(C_E1), op=ALU.subtract),
                  reads=[('sm', C_E1), ('sm', C_E2)], writes=[('sm', C_NLAM)])
            SC.op('dve', lambda e: e.tensor_scalar(out=col(C_NLAM), in0=col(C_NLAM), scalar1=-lambda_init,
                                                   scalar2=None, op0=ALU.add),
                  reads=[('sm', C_NLAM)], writes=[('sm', C_NLAM)])
            const_dma(col(C_GS), subg_d[l, :].rearrange("(p o) -> p o", o=1), ('sm', C_GS))
            SC.op('dve', lambda e: e.tensor_scalar(out=col(C_GS), in0=col(C_GS), scalar1=1.0 - lambda_init,
                                                   scalar2=None, op0=ALU.mult),
                  reads=[('sm', C_GS)], writes=[('sm', C_GS)])
            const_dma(col(C_GKV), kvg_d[l, :].rearrange("(p o) -> p o", o=1), ('sm', C_GKV))
            const_dma(col(C_GQ0), qg_d[l, 0:128].rearrange("(p o) -> p o", o=1), ('sm', C_GQ0))
            const_dma(col(C_GQ1), qg_d[l, 128:256].rearrange("(p o) -> p o", o=1), ('sm', C_GQ1))

        def load_v_weights(l):
            load_wbig(lambda g: [(wb4(g), w_in_d[l, g * 512:(g + 1) * 512, 1024:1536].rearrange("(c p) n -> p c n", p=128))], 'v')

        def load_wqk(l, h):
            for part, c0 in ((0, h * 128), (1, 512 + h * 128)):
                wload(Wqk[:, :, part * 128:(part + 1) * 128],
                      w_in_d[l, :, c0:c0 + 128].rearrange("(c p) n -> p c n", p=128), ('Wqk', part), 'wqk%d' % part)

        def diff_v(l):
            for t in range(NT_):
                b = ps_any()

                def f(e, t=t, b=b):
                    r = None
                    for c in range(8):
                        r = e.matmul(PS[b][:, :], lhsT=xT[:, c, t * 128:(t + 1) * 128], rhs=Wbig[:, c, :],
                                     start=(c == 0), stop=(c == 7))
                    return r
                SC.op('pe', f, reads=[('xT', t // 4), ('Wbig', 0), ('Wbig', 1)], writes=[('ps', b)])
                evac(VU[:, t * 512:(t + 1) * 512], PS[b][:, :], [('ps', b)], [('VU', t)])

        def diff_head(l, h):
            for tb in range(4):
                b = ps_any()

                def f(e, tb=tb, b=b):
                    r = None
                    for c in range(8):
                        r = e.matmul(PS[b][:, :], lhsT=Wqk[:, c, 0:128], rhs=xT_blk(c, tb), start=(c == 0), stop=(c == 7))
                    return r
                SC.op('pe', f, reads=[('xT', tb), ('Wqk', 0)], writes=[('ps', b)])
                SC.op('act', lambda e, tb=tb, b=b: e.copy(out=Qs[0:64, tb * 512:(tb + 1) * 512], in_=PS[b][0:64, :]),
                      reads=[('ps', b)], writes=[('Q', tb)])
                SC.op('dve', lambda e, tb=tb, b=b: e.tensor_copy(out=ckvT[64:128, tb * 512:(tb + 1) * 512], in_=PS[b][64:128, :]),
                      reads=[('ps', b)], writes=[('ckv', tb)])
            proj_fm(lambda c: Wqk[:, c, 128:256], xT_blk, 8, Ks, 'K', [lambda tb: ('xT', tb), ('Wqk', 1)])
            if h < 3:
                load_wqk(l, h + 1)
                load_wo(l, (h + 1) * 128, (h + 1) % 3)
            else:
                load_wo(l, 512, 4 % 3)
            attention('diff', h, h % 3)

        def load_mla_win(l):
            load_wbig(lambda g: [
                (wb4(g)[:, :, 0:448], w_in_d[l, g * 512:(g + 1) * 512, 1536:1984].rearrange("(c p) n -> p c n", p=128)),
                (wb4(g)[:, :, 448:512], krsw_d[l, g * 512:(g + 1) * 512, :].rearrange("(c p) n -> p c n", p=128))], 'm')

        def load_mla_small(l):
            wukv4 = wukv_d[l, :, :].rearrange("p (h n) -> p h n", h=4)
            wload(Wkn[:, :].rearrange("p (h n) -> p h n", h=4), wukv4[:, :, 0:128], ('Wkn',), 'wkn')
            wload(Wmv[:, :].rearrange("p (h n) -> p h n", h=4), wukv4[:, :, 128:256], ('Wmv',), 'wmv')
            for j in range(2):
                wload(Wuq[:, j, 0:768], wuq_d[l, j * 128:(j + 1) * 128, :], ('Wuq', j), 'wuq%d' % j)
                wload(Wuq[:, j, 768:1024], wuqsw_d[l, j * 128:(j + 1) * 128, :], ('Wuq', j), 'wuq%d' % j)

        def rms_block(banks, nfeat, rk, out_fn, okeys, gcols):
            n = len(banks)
            for j, b in enumerate(banks):
                SC.op('act', lambda e, j=j, b=b: e.activation(out=Tb[j][:, :], in_=PS[b][:, :], func=AF.Square),
                      reads=[('ps', b)], writes=[('T', j)])
            bs = ps_any()

            def f(e):
                r = None
                for j in range(n):
                    r = e.matmul(PS[bs][:, :], lhsT=ones32[:, :], rhs=Tb[j][:, :], start=(j == 0), stop=(j == n - 1))
                return r
            SC.op('pe', f, reads=[('T', j) for j in range(n)] + [('ones32',)], writes=[('ps', bs)])
            SC.op('act', lambda e: e.activation(out=Rb[rk][:, :], in_=PS[bs][:, :], func=AF.Ln,
                                                bias=col(C_EPSR), scale=1.0 / nfeat),
                  reads=[('ps', bs), ('sm', C_EPSR)], writes=[('R', rk)])
            SC.op('act', lambda e: e.activation(out=Rb[rk][:, :], in_=Rb[rk][:, :], func=AF.Exp, scale=-0.5),
                  reads=[('R', rk)], writes=[('R', rk)])
            for j, b in enumerate(banks):
                SC.op('dve', lambda e, j=j, b=b: e.scalar_tensor_tensor(out=out_fn(j), in0=PS[b][:, :], scalar=col(gcols[j]),
                                                                        in1=Rb[rk][:, :], op0=ALU.mult, op1=ALU.mult),
                      reads=[('ps', b), ('R', rk), ('sm', gcols[j])], writes=[okeys[j]])

        def mm8(b, c0, w, tb, m=128):
            def f(e):
                r = None
                for c in range(8):
                    r = e.matmul(PS[b][0:m, :], lhsT=Wbig[:, c, c0:c0 + w], rhs=xT_blk(c, tb), start=(c == 0), stop=(c == 7))
                return r
            SC.op('pe', f, reads=[('xT', tb), ('Wbig', 0), ('Wbig', 1)], writes=[('ps', b)])

        def mla_pre(tb):
            tsl = slice(tb * 512, (tb + 1) * 512)
            bq = [ps_any(), ps_any()]
            for j in range(2):
                mm8(bq[j], j * 128, 128, tb)
            rms_block(bq, 256.0, 0, lambda j: cqT[:, j, tsl], [('CW', tb), ('CW', 4 + tb)], [C_GQ0, C_GQ1])
            bk = ps_any()
            mm8(bk, 256, 128, tb)
            rms_block([bk], 128.0, 1, lambda j: ckvT[:, tsl], [('ckv', tb)], [C_GKV])
            ba, bb = ps_any(), ps_any()
            mm8(ba, 384, 64, tb, m=64)
            mm8(bb, 448, 64, tb, m=64)
            rope_block(ba, bb, tb, krT[0:64, tsl], ('kr', tb))

        def mla_mv():
            for t in range(NT_):
                b = ps_any()
                SC.op('pe', lambda e, t=t, b=b: e.matmul(PS[b][:, :], lhsT=ckvT[:, t * 128:(t + 1) * 128], rhs=Wmv[:, :],
                                                         start=True, stop=True),
                      reads=[('ckv', t // 4), ('Wmv',)], writes=[('ps', b)])
                evac(VU[:, t * 512:(t + 1) * 512], PS[b][:, :], [('ps', b)], [('VU', t)])

        def mla_head(l, h):
            proj_fm(lambda c: Wkn[:, h * 128:(h + 1) * 128], lambda c, tb: ckvT[:, tb * 512:(tb + 1) * 512], 1, Ks, 'K',
                    [lambda tb: ('ckv', tb), ('Wkn',)])
            proj_fm(lambda c: Wuq[:, c, h * 192:h * 192 + 128], lambda c, tb: cqT[:, c, tb * 512:(tb + 1) * 512], 2, Qs, 'Q',
                    [lambda tb: ('CW', tb), lambda tb: ('CW', 4 + tb), ('Wuq', 0), ('Wuq', 1)])

            def qr_blk(tb):
                tsl = slice(tb * 512, (tb + 1) * 512)
                ba, bb = ps_any(), ps_any()
                for (b, c0) in ((ba, h * 192 + 128), (bb, 768 + h * 64)):
                    def f(e, b=b, c0=c0):
                        e.matmul(PS[b][0:64, :], lhsT=Wuq[:, 0, c0:c0 + 64], rhs=cqT[:, 0, tsl], start=True, stop=False)
                        return e.matmul(PS[b][0:64, :], lhsT=Wuq[:, 1, c0:c0 + 64], rhs=cqT[:, 1, tsl],
                                        start=False, stop=True)
                    SC.op('pe', f, reads=[('CW', tb), ('CW', 4 + tb), ('Wuq', 0), ('Wuq', 1)], writes=[('ps', b)])
                rope_block(ba, bb, tb, QR[0:64, tsl], ('QR', tb))
            for tb in range(4):
                qr_blk(tb)
            if h < 3:
                load_wo(l, 512 + (h + 1) * 128, (4 + h + 1) % 3)
            else:
                load_w2(l, 0)
            attention('mla', h, (4 + h) % 3)

        def load_w1(l, fg):
            load_wbig(lambda g: [(wb4(g), w1_d[l, g * 512:(g + 1) * 512, fg * 512:(fg + 1) * 512].rearrange("(c p) n -> p c n", p=128))], 'w1')

        def load_w2(l, fg):
            for g in range(2):
                r0 = fg * 512 + g * 256
                SC.dma('pool', 'w2_%d' % g, lambda e, g=g, r0=r0: e.dma_start(
                    out=W2[:, g * 2:(g + 1) * 2, :], in_=w2_d[l, r0:r0 + 256, :].rearrange("(c p) n -> p c n", p=128)),
                    writes=[('CW', 4 * g + i) for i in range(4)])

        def mlp_u(j, tb):
            b = ps_any()
            mm8(b, j * 128, 128, tb)
            k = (j * 4 + tb) % 2
            SC.op('act', lambda e: e.activation(out=Tb[k][:, :], in_=PS[b][:, :], func=AF.Relu),
                  reads=[('ps', b)], writes=[('T', k)])
            uc = j * 2048 + tb * 512
            SC.op('act', lambda e: e.activation(out=VU[:, uc:uc + 512], in_=Tb[k][:, :], func=AF.Square),
                  reads=[('T', k)], writes=[('VU', j * 4 + tb)])

        def mlp_group(l, fg, ln=None, u_done=False):
            if not u_done:
                for j in range(4):
                    for tb in range(4):
                        mlp_u(j, tb)
            if fg < 7:
                load_w1(l, fg + 1)
            elif l + 1 < DEPTH and STOP is None:
                load_v_weights(l + 1)
            for t in range(NT_):
                for hf in range(2):
                    b = ps_any()

                    def f(e, t=t, hf=hf, b=b):
                        r = None
                        for j in range(4):
                            uc = j * 2048 + t * 128
                            r = e.matmul(PS[b][:, :], lhsT=VU[:, uc:uc + 128], rhs=W2[:, j, hf * 512:(hf + 1) * 512],
                                         start=(j == 0), stop=(j == 3))
                        return r
                    SC.op('pe', f, reads=[('VU', j * 4 + t // 4) for j in range(4)] + [('CW', i) for i in range(8)],
                          writes=[('ps', b)])
                    SC.op('dve', lambda e, t=t, hf=hf, b=b: e.tensor_tensor(
                        out=x_tm[:, t, hf * 512:(hf + 1) * 512], in0=PS[b][:, :],
                        in1=x_tm[:, t, hf * 512:(hf + 1) * 512], op=ALU.add),
                        reads=[('ps', b), ('x', t, hf)], writes=[('x', t, hf)])
                if ln is not None:
                    ln.push(t)
            if fg < 7:
                load_w2(l, fg + 1)

        for l in range(DEPTH):
            layer_consts(l)
            if STOP == 'xT':
                break
            if l == 0:
                load_v_weights(l)
            load_wqk(l, 0)
            load_wo(l, 0, 0)
            load_mla_small(l)
            SC.op('pool', lambda e: e.memset(Qs[64:128, :], 0.0), writes=[('Q', tb) for tb in range(4)])
            SC.op('pool', lambda e: e.memset(ckvT[0:64, :], 0.0), writes=[('ckv', tb) for tb in range(4)])
            diff_v(l)
            load_mla_win(l)
            for h in range(4):
                diff_head(l, h)
            pend_drain()
            if STOP == 'diff':
                break
            for tb in range(4):
                mla_pre(tb)
            load_w1(l, 0)
            mla_mv()
            for h in range(3):
                mla_head(l, h)
            if STOP in ('attn', 'ln1'):
                mla_head(l, 3)
                pend_drain()
                if STOP == 'attn':
                    break
                layer_norm(ln1g_d, ln1b_d, l, True)
                break
            mla_head(l, 3)
            pend_drain()
            ln1 = LNPipe(ln1g_d, ln1b_d, l, False)
            for tb in range(4):
                for t in range(4 * tb, 4 * tb + 4):
                    ln1.push(t)
                if tb >= 1:
                    for j in range(4):
                        mlp_u(j, tb - 1)
            ln1.flush()
            for j in range(4):
                mlp_u(j, 3)
            ln2 = LNPipe(ln2g_d, ln2b_d, l, l == DEPTH - 1 or STOP == 'l0')
            for fg in range(8):
                mlp_group(l, fg, ln2 if fg == 7 else None, u_done=(fg == 0))
            ln2.flush()
            if STOP == 'l0':
                break
        if STOP in ('xT', 'diff', 'attn'):
            for t in range(NT_):
                SC.dma('sp', 'out', lambda e, t=t: e.dma_start(out=out_d[t * 128:(t + 1) * 128, :], in_=x_tm[:, t, :]),
                       reads=[('x', t, 0), ('x', t, 1)])

        sem_keys = list(SC.cnt.keys())
        sems = {k: es.enter_context(nc.semaphore("s_%s" % str(k))) for k in sem_keys}
        out_total = SC.cnt['out']
        with nc.Block() as block:
            @block.tensor
            def _(eng):
                SC.replay('pe', eng, sems)

            @block.scalar
            def _(eng):
                SC.replay('act', eng, sems)

            @block.vector
            def _(eng):
                SC.replay('dve', eng, sems)

            @block.gpsimd
            def _(eng):
                SC.replay('pool', eng, sems)

            @block.sync
            def _(eng):
                SC.replay('sp', eng, sems)
                eng.wait_ge(sems['out'], out_total)
    return nc


def _host_consts():
    k = np.arange(128)[:, None]
    j = np.arange(256)[None, :]
    rel = k - j
    nb = 16
    ret = (rel > 0).astype(np.int32) * nb
    n = np.abs(rel)
    max_exact = nb // 2
    nf = np.maximum(n, 1).astype(np.float32)
    large = max_exact + (np.log(nf / max_exact) / math.log(128 / max_exact) * (nb - max_exact)).astype(np.int32)
    large = np.minimum(large, nb - 1)
    bucket = ret + np.where(n < max_exact, n, large)
    masked = (k >= 64) & (j < 64)
    pos = np.arange(S, dtype=np.float32)
    inv = (10000.0 ** (-np.arange(0, 64, 2, dtype=np.float32) / 64)).astype(np.float32)
    ang = pos[None, :] * inv[:, None]
    cos = np.cos(ang).astype(np.float32)
    sin = np.sin(ang).astype(np.float32)
    cos2 = np.concatenate([cos, cos], axis=0)
    sin2s = np.concatenate([-sin, sin], axis=0)
    mask_tile = np.where((np.arange(128)[:, None] >= 64) & (np.arange(128)[None, :] < 64), NEG, 0.0).astype(np.float32)
    return bucket, masked, np.ascontiguousarray(cos2), np.ascontiguousarray(sin2s), mask_tile


_NC_CACHE = {}


def kernel(x, w_in, lambda_q1, lambda_k1, lambda_q2, lambda_k2, subln_g, q_norm_g, w_uq, kv_norm_g,
           w_ukv, rel_bias, w_o, ln1_g, ln1_b, w_mlp_in, w_mlp_out, ln2_g, ln2_b):
    f = lambda a: np.ascontiguousarray(np.asarray(a, dtype=np.float32))
    x = f(x)
    w_in = f(w_in)
    w_uq = f(w_uq)
    rel_bias = f(rel_bias)
    bucket, masked, cos2, sin2s, mask_tile = _host_consts()
    kr = w_in[:, :, 1920:1984]
    krsw = np.ascontiguousarray(np.concatenate([kr[:, :, 32:64], kr[:, :, 0:32]], axis=2))
    uq4 = w_uq.reshape(DEPTH, 256, 4, 192)[:, :, :, 128:192]
    uqsw = np.ascontiguousarray(np.concatenate([uq4[..., 32:64], uq4[..., 0:32]], axis=3).reshape(DEPTH, 256, 256))
    bt = np.transpose(rel_bias[bucket], (2, 0, 1))
    bt = np.ascontiguousarray(np.where(masked[None], np.float32(NEG), bt).astype(np.float32))
    shared = {
        "w_in": w_in, "w_in_krsw": krsw, "lambda_q1": f(lambda_q1), "lambda_k1": f(lambda_k1),
        "lambda_q2": f(lambda_q2), "lambda_k2": f(lambda_k2), "subln_g": f(subln_g), "q_norm_g": f(q_norm_g),
        "w_uq": w_uq, "w_uq_sw": uqsw, "kv_norm_g": f(kv_norm_g), "w_ukv": f(w_ukv), "rel_bias": rel_bias,
        "bias_tiles": bt, "mask_tile": mask_tile, "w_o": f(w_o), "ln1_g": f(ln1_g), "ln1_b": f(ln1_b),
        "w_mlp_in": f(w_mlp_in), "w_mlp_out": f(w_mlp_out), "ln2_g": f(ln2_g), "ln2_b": f(ln2_b),
        "cos2T": cos2, "sin2sT": sin2s, "ident": np.eye(128, dtype=np.float32),
    }
    if 'nc' not in _NC_CACHE:
        _NC_CACHE['nc'] = build_program()
    nc = _NC_CACHE['nc']
    in_maps = []
    for b in range(8):
        m = dict(shared)
        m["x"] = np.ascontiguousarray(x[b])
        in_maps.append(m)
    res = run_bass_kernel_spmd(nc, in_maps, core_ids=list(range(8)))
    return np.stack([np.asarray(r["out"], dtype=np.float32) for r in res.results], axis=0)
```

```python
import math
import os
from contextlib import ExitStack

import numpy as np
import concourse.bass as bass
import concourse.mybir as mybir
from concourse.bass_utils import run_bass_kernel_spmd

F32 = mybir.dt.float32
BF16 = mybir.dt.bfloat16
AF = mybir.ActivationFunctionType
ALU = mybir.AluOpType

D = 1024
S = 2048
DEPTH = 2
NT_ = S // 128
IN_W = 1984
DFF = 4096
ALPHA = (2 * DEPTH) ** 0.25
LN_EPS = 1e-5
RMS_EPS = 1e-6
ROPE_DB = True
NEG = -30000.0
DIFF_SCALE = 64 ** -0.5
MLA_SCALE = 192 ** -0.5


class Sched:
    ENG = ('pe', 'act', 'dve', 'pool', 'sp')

    def __init__(self):
        self.ops = {e: [] for e in self.ENG}
        self.cnt = {}
        self.known = {e: {} for e in self.ENG}
        self.snap = {}
        self.res = {}

    def _learn(self, eng, s, v):
        kn = self.known[eng]
        if kn.get(s, 0) < v:
            kn[s] = v
        sn = self.snap.get((s, v))
        if sn:
            for s2, v2 in sn.items():
                if kn.get(s2, 0) < v2:
                    kn[s2] = v2

    def _deps(self, eng, reads, writes, after=()):
        deps = {}

        def add(s, v):
            if deps.get(s, 0) < v:
                deps[s] = v
        for r in reads:
            st = self.res.get(r)
            if st:
                for s, v in st[0].items():
                    if eng == 'pe' and s == 'pe':
                        continue
                    add(s, v)
        for w in writes:
            st = self.res.get(w)
            if st:
                for s, v in st[0].items():
                    if eng == 'pe' and s == 'pe':
                        continue
                    add(s, v)
                for s, v in st[1].items():
                    if s == eng and eng == 'pe':
                        continue
                    add(s, v)
        for w in after:
            st = self.res.get(w)
            if st:
                for s, v in st[1].items():
                    if s == eng and eng == 'pe':
                        continue
                    add(s, v)
        kn = self.known[eng]
        waits = [(s, v) for s, v in deps.items() if kn.get(s, 0) < v]
        for s, v in waits:
            self._learn(eng, s, v)
        return waits

    def _mark(self, ev, reads, writes):
        s, v = ev
        for r in reads:
            self.res.setdefault(r, ({}, {}))[1][s] = v
        for w in writes:
            self.res.setdefault(w, ({}, {}))[0][s] = v

    def op(self, eng, fn, reads=(), writes=(), after=()):
        waits = self._deps(eng, reads, writes, after)
        self.cnt[eng] = self.cnt.get(eng, 0) + 1
        ev = (eng, self.cnt[eng])
        sn = dict(self.known[eng])
        sn[eng] = ev[1]
        self.snap[ev] = sn
        self.ops[eng].append((waits, fn, (eng, 1)))
        self._mark(ev, reads, writes)

    def dma(self, q, semkey, fn, reads=(), writes=()):
        waits = self._deps(q, reads, writes)
        self.cnt[semkey] = self.cnt.get(semkey, 0) + 16
        ev = (semkey, self.cnt[semkey])
        self.snap[ev] = dict(self.known[q])
        self.ops[q].append((waits, fn, (semkey, 16)))
        self._mark(ev, reads, writes)

    def replay(self, name, eng, sems):
        for waits, fn, inc in self.ops[name]:
            for s, v in waits:
                eng.wait_ge(sems[s], v)
            r = fn(eng)
            r.then_inc(sems[inc[0]], inc[1])


def build_program(debug_stop=None):
    STOP = debug_stop
    nc = bass.Bass("TRN2", target_bir_lowering=False)
    SC = Sched()

    def din(name, shape):
        return nc.dram_tensor(name, list(shape), F32, kind="ExternalInput").ap()

    x_d = din("x", (S, D))
    w_in_d = din("w_in", (DEPTH, D, IN_W))
    krsw_d = din("w_in_krsw", (DEPTH, D, 64))
    lq1_d = din("lambda_q1", (DEPTH, 64))
    lk1_d = din("lambda_k1", (DEPTH, 64))
    lq2_d = din("lambda_q2", (DEPTH, 64))
    lk2_d = din("lambda_k2", (DEPTH, 64))
    subg_d = din("subln_g", (DEPTH, 128))
    qg_d = din("q_norm_g", (DEPTH, 256))
    wuq_d = din("w_uq", (DEPTH, 256, 768))
    wuqsw_d = din("w_uq_sw", (DEPTH, 256, 256))
    kvg_d = din("kv_norm_g", (DEPTH, 128))
    wukv_d = din("w_ukv", (DEPTH, 128, 1024))
    relb_d = din("rel_bias", (32, 4))
    bt_d = din("bias_tiles", (4, 128, 256))
    mask_d = din("mask_tile", (128, 128))
    wo_d = din("w_o", (DEPTH, D, D))
    ln1g_d = din("ln1_g", (DEPTH, D))
    ln1b_d = din("ln1_b", (DEPTH, D))
    w1_d = din("w_mlp_in", (DEPTH, D, DFF))
    w2_d = din("w_mlp_out", (DEPTH, DFF, D))
    ln2g_d = din("ln2_g", (DEPTH, D))
    ln2b_d = din("ln2_b", (DEPTH, D))
    cos_d = din("cos2T", (64, S))
    sin_d = din("sin2sT", (64, S))
    ident_d = din("ident", (128, 128))
    out_d = nc.dram_tensor("out", [S, D], F32, kind="ExternalOutput").ap()

    es = ExitStack()

    def sb(name, shape, dt):
        return es.enter_context(nc.sbuf_tensor(name, list(shape), dt))

    with es:
        x_tm = sb("x_tm", (128, NT_, D), F32)
        lnG = sb("lnG", (128, D), F32)
        lnB = sb("lnB", (128, D), F32)
        Rb = [sb("R0", (128, 512), F32), sb("R1", (128, 512), F32)]
        Tb = [sb("T0", (128, 512), F32), sb("T1", (128, 512), F32)]
        cosb2 = [sb("cosb_a", (64, 512), F32), sb("cosb_b", (64, 512), F32)]
        sinb2 = [sb("sinb_a", (64, 512), F32), sb("sinb_b", (64, 512), F32)]
        ident = sb("ident32", (128, 128), F32)
        ones32 = sb("ones32", (128, 128), F32)
        small = sb("small", (128, 64), F32)
        lamt = sb("lamt", (128, 4, 64), F32)
        bnst = sb("bnst", (128, 48), F32)

        xT = sb("xT", (128, 8, S), BF16)
        VU = sb("VU", (128, 8192), BF16)
        Qs = sb("Qs", (128, S), BF16)
        Ks = sb("Ks", (128, S), BF16)
        QR = sb("QR", (128, S), BF16)
        CW = sb("CW", (128, 4096), BF16)
        ckvT = sb("ckvT", (128, S), BF16)
        krT = sb("krT", (128, S), BF16)
        mixT = sb("mixT", (128, S), BF16)
        Wqk = sb("Wqk", (128, 8, 256), BF16)
        Wbig = sb("Wbig", (128, 8, 512), BF16)
        Wuq = sb("Wuq", (128, 2, 1024), BF16)
        Wkn = sb("Wkn", (128, 512), BF16)
        Wmv = sb("Wmv", (128, 512), BF16)
        Wo2 = [sb("Wo_a", (128, D), BF16), sb("Wo_b", (128, D), BF16), sb("Wo_c", (128, D), BF16)]
        Eb = [sb("E%d" % i, (128, 512), BF16) for i in range(4)]
        ones16 = sb("ones16", (128, 128), BF16)
        ident16 = sb("ident16", (128, 128), BF16)
        bt_hi = sb("bt_hi", (128, 4, 256), BF16)
        bt_lo = sb("bt_lo", (128, 4, 256), BF16)
        mk16 = sb("mk16", (128, 128), BF16)
        PS = [es.enter_context(nc.psum_tensor("ps%d" % i, [128, 512], F32)) for i in range(8)]

        cqT = CW[:, :].rearrange("p (j n) -> p j n", j=2)
        W2 = CW[:, :].rearrange("p (j n) -> p j n", j=4)

        C_EPSR, C_EPSL, C_NLAM, C_GS, C_GKV, C_GQ0, C_GQ1, C_CB = 0, 1, 2, 3, 4, 5, 6, 8
        C_E1, C_E2, C_MV, C_RSTD, C_NMR = 12, 13, 20, 22, 23

        def col(c):
            return small[:, c:c + 1]

        def wload(dst_ap, src_ap, wkey, semkey):
            SC.dma('pool', semkey, lambda e: e.dma_start(out=dst_ap, in_=src_ap), writes=[wkey])

        ps_rr = [0]

        def ps_any():
            b = ps_rr[0] % 8
            ps_rr[0] += 1
            return b

        misc_rr = [0]

        def ps_misc():
            b = 7
            misc_rr[0] += 1
            return b

        ev_rr = [0]

        def evac(out_ap, in_ap, reads, writes):
            k = ev_rr[0] % 2
            ev_rr[0] += 1
            if k == 0:
                SC.op('act', lambda e: e.copy(out=out_ap, in_=in_ap), reads=reads, writes=writes)
            else:
                SC.op('dve', lambda e: e.tensor_copy(out=out_ap, in_=in_ap), reads=reads, writes=writes)

        cdma_i = [0]

        def const_dma(out_ap, in_ap, wkey):
            cdma_i[0] += 1
            SC.dma('sp', 'c%d' % cdma_i[0], lambda e: e.dma_start(out=out_ap, in_=in_ap), writes=[wkey])

        const_dma(ident[:, :], ident_d[:, :], ('ident',))
        ntile = lnG[:, 0:1024].rearrange("p (h n) -> p h n", h=4)
        maskt = lnB[:, 0:128]
        const_dma(maskt, mask_d[:, :], ('lnB',))
        const_dma(ntile, bt_d.rearrange("h p n -> p h n"), ('lnG',))
        SC.op('pool', lambda e: e.memset(ones32[:, :], 1.0), writes=[('ones32',)])
        SC.op('pool', lambda e: e.memset(ones16[:, :], 1.0), writes=[('ones16',)])
        SC.op('pool', lambda e: e.memset(QR[64:128, :], 0.0), writes=[('QR', tb) for tb in range(4)])
        SC.op('pool', lambda e: e.memset(krT[64:128, :], 0.0), writes=[('kr', tb) for tb in range(4)])
        SC.op('pool', lambda e: e.memset(col(C_EPSR), RMS_EPS), writes=[('sm', C_EPSR)])
        SC.op('pool', lambda e: e.memset(col(C_EPSL), LN_EPS), writes=[('sm', C_EPSL)])
        for h in range(4):
            const_dma(col(C_CB + h), relb_d[15, h:h + 1].partition_broadcast(128), ('sm', C_CB + h))

        SC.op('dve', lambda e: e.tensor_copy(out=ident16[:, :], in_=ident[:, :]), reads=[('ident',)], writes=[('ident16',)])
        SC.op('dve', lambda e: e.tensor_scalar(out=mk16[:, :], in0=maskt, scalar1=1.0 / MLA_SCALE, scalar2=None,
                                               op0=ALU.mult), reads=[('lnB',)], writes=[('bt16',)])
        for h in range(4):
            SC.op('dve', lambda e, h=h: e.tensor_scalar(out=ntile[:, h, :], in0=ntile[:, h, :], scalar1=col(C_CB + h),
                                                        scalar2=None, op0=ALU.subtract),
                  reads=[('lnG',), ('sm', C_CB + h)], writes=[('lnG',)])
            SC.op('dve', lambda e, h=h: e.tensor_scalar(out=ntile[:, h, :], in0=ntile[:, h, :], scalar1=1.0 / DIFF_SCALE,
                                                        scalar2=None, op0=ALU.mult),
                  reads=[('lnG',)], writes=[('lnG',)])
            SC.op('dve', lambda e, h=h: e.tensor_copy(out=bt_hi[:, h, :], in_=ntile[:, h, :]),
                  reads=[('lnG',)], writes=[('bt16',)])
            SC.op('dve', lambda e, h=h: e.tensor_tensor(out=bt_lo[:, h, :], in0=ntile[:, h, :], in1=bt_hi[:, h, :],
                                                        op=ALU.subtract),
                  reads=[('lnG',), ('bt16',)], writes=[('bt16',)])

        for t in range(NT_):
            SC.dma('sp', 'xl%d' % t,
                   lambda e, t=t: e.dma_start(out=x_tm[:, t, :], in_=x_d[t * 128:(t + 1) * 128, :]),
                   writes=[('x', t, 0), ('x', t, 1)])

        def make_xT(t, prescaled=False, inflight=False):
            for g in range(2):
                b = ps_misc() if inflight else ps_any()

                def f(e, t=t, g=g, b=b):
                    r = None
                    for j in range(4):
                        c = g * 4 + j
                        r = e.transpose(PS[b][:, j * 128:(j + 1) * 128], x_tm[:, t, c * 128:(c + 1) * 128], ident[:, :])
                    return r
                SC.op('pe', f, reads=[('x', t, g), ('ident',)], writes=[('ps', b)])
                if prescaled and inflight:
                    SC.op('dve', lambda e, t=t, g=g, b=b: e.tensor_scalar(
                        out=xT[:, g * 4:(g + 1) * 4, t * 128:(t + 1) * 128],
                        in0=PS[b][:, :].rearrange("p (j n) -> p j n", j=4), scalar1=1.0 / ALPHA, scalar2=None,
                        op0=ALU.mult),
                        reads=[('ps', b)], writes=[('xT', t // 4)])
                elif prescaled:
                    SC.op('act', lambda e, t=t, g=g, b=b: e.mul(
                        out=xT[:, g * 4:(g + 1) * 4, t * 128:(t + 1) * 128],
                        in_=PS[b][:, :].rearrange("p (j n) -> p j n", j=4), mul=1.0 / ALPHA),
                        reads=[('ps', b)], writes=[('xT', t // 4)])
                else:
                    SC.op('dve', lambda e, t=t, g=g, b=b: e.tensor_copy(
                        out=xT[:, g * 4:(g + 1) * 4, t * 128:(t + 1) * 128],
                        in_=PS[b][:, :].rearrange("p (j n) -> p j n", j=4)),
                        reads=[('ps', b)], writes=[('xT', t // 4)])
            if not prescaled:
                SC.op('act', lambda e, t=t: e.mul(out=x_tm[:, t, :], in_=x_tm[:, t, :], mul=ALPHA),
                      reads=[('x', t, 0), ('x', t, 1)], writes=[('x', t, 0), ('x', t, 1)])

        for t in range(NT_):
            make_xT(t)

        def rope_block(ba, bb, tb, out_ap, wkey):
            k = (tb % 2) if ROPE_DB else 0
            cosb, sinb = cosb2[k], sinb2[k]
            (t0, t0k), (t1, t1k) = ((Tb[0], ('T', 0)), (Tb[1], ('T', 1)))
            SC.dma('sp', 'ropec%d' % k, lambda e: e.dma_start(out=cosb[:, :], in_=cos_d[:, tb * 512:(tb + 1) * 512]),
                   writes=[('cosb', k)])
            SC.dma('sp', 'ropes%d' % k, lambda e: e.dma_start(out=sinb[:, :], in_=sin_d[:, tb * 512:(tb + 1) * 512]),
                   writes=[('sinb', k)])
            SC.op('dve', lambda e: e.tensor_tensor(out=t0[0:64, :], in0=PS[ba][0:64, :], in1=cosb[:, :], op=ALU.mult),
                  reads=[('ps', ba), ('cosb', k)], writes=[t0k])
            SC.op('dve', lambda e: e.tensor_tensor(out=t1[0:64, :], in0=PS[bb][0:64, :], in1=sinb[:, :], op=ALU.mult),
                  reads=[('ps', bb), ('sinb', k)], writes=[t1k])
            SC.op('pool', lambda e: e.tensor_tensor(out=out_ap, in0=t0[0:64, :], in1=t1[0:64, :], op=ALU.add),
                  reads=[t0k, t1k], writes=[wkey])

        def wo_accum(sp, wi):
            for tt in range(4):
                t = sp * 4 + tt
                for hf in range(2):
                    mb = ps_misc()
                    SC.op('pe', lambda e, t=t, hf=hf, mb=mb: e.matmul(
                        PS[mb][:, :], lhsT=mixT[:, t * 128:(t + 1) * 128], rhs=Wo2[wi][:, hf * 512:(hf + 1) * 512],
                        start=True, stop=True),
                        reads=[('mix', sp), ('Wo', wi)], writes=[('ps', mb)])
                    SC.op('dve', lambda e, t=t, hf=hf, mb=mb: e.tensor_tensor(
                        out=x_tm[:, t, hf * 512:(hf + 1) * 512], in0=PS[mb][:, :],
                        in1=x_tm[:, t, hf * 512:(hf + 1) * 512], op=ALU.add),
                        reads=[('ps', mb), ('x', t, hf)], writes=[('x', t, hf)])
                    yield 'work'

        D1, D2 = 4, 5

        def half_post(k):
            SC.op('act', lambda e: e.activation(out=Rb[k][:, :], in_=PS[5 + k][:, :], func=AF.Ln),
                  reads=[('ps', 5 + k)], writes=[('R', k)])
            SC.op('act', lambda e: e.activation(out=Rb[k][:, :], in_=Rb[k][:, :], func=AF.Exp, scale=-1.0),
                  reads=[('R', k)], writes=[('R', k)])
            SC.op('dve', lambda e: e.tensor_tensor(out=Tb[k][:, :], in0=PS[3 + k][:, :], in1=Rb[k][:, :], op=ALU.mult),
                  reads=[('ps', 3 + k), ('R', k)], writes=[('T', k)])

        def post_diff(sp, wi, state):
            SC.op('dve', lambda e: e.scalar_tensor_tensor(out=Tb[0][:, :], in0=Tb[1][:, :], scalar=col(C_NLAM),
                                                          in1=Tb[0][:, :], op0=ALU.mult, op1=ALU.add),
                  reads=[('T', 0), ('T', 1), ('sm', C_NLAM)], writes=[('T', 0)])
            SC.op('pool', lambda e: e.tensor_tensor(out=Rb[0][:, :], in0=Tb[0][:, :], in1=Tb[0][:, :], op=ALU.mult),
                  reads=[('T', 0)], writes=[('R', 0)])
            for _ in range(D1):
                yield 'delay'
            mb = ps_misc()
            SC.op('pe', lambda e: e.matmul(PS[mb][:, :], lhsT=ones32[:, :], rhs=Rb[0][:, :], start=True, stop=True),
                  reads=[('R', 0), ('ones32',)], writes=[('ps', mb)])
            SC.op('act', lambda e: e.activation(out=Rb[1][:, :], in_=PS[mb][:, :], func=AF.Ln,
                                                bias=col(C_EPSR), scale=1.0 / 128.0),
                  reads=[('ps', mb), ('sm', C_EPSR)], writes=[('R', 1)])
            SC.op('act', lambda e: e.activation(out=Rb[1][:, :], in_=Rb[1][:, :], func=AF.Exp, scale=-0.5),
                  reads=[('R', 1)], writes=[('R', 1)])
            SC.op('dve', lambda e: e.scalar_tensor_tensor(out=mixT[:, sp * 512:(sp + 1) * 512], in0=Tb[0][:, :],
                                                          scalar=col(C_GS), in1=Rb[1][:, :],
                                                          op0=ALU.mult, op1=ALU.mult),
                  reads=[('T', 0), ('R', 1), ('sm', C_GS)], writes=[('mix', sp)])
            state['released'] = True
            for _ in range(D2):
                yield 'delay'
            yield from wo_accum(sp, wi)

        def post_mla(sp, wi, state):
            ob, smb = 3 + sp % 2, 5 + sp % 2
            k = sp % 2
            SC.op('act', lambda e: e.activation(out=Rb[k][:, :], in_=PS[smb][:, :], func=AF.Ln),
                  reads=[('ps', smb)], writes=[('R', k)])
            SC.op('act', lambda e: e.activation(out=Rb[k][:, :], in_=Rb[k][:, :], func=AF.Exp, scale=-1.0),
                  reads=[('R', k)], writes=[('R', k)])
            SC.op('dve', lambda e: e.tensor_tensor(out=mixT[:, sp * 512:(sp + 1) * 512], in0=PS[ob][:, :],
                                                   in1=Rb[k][:, :], op=ALU.mult),
                  reads=[('ps', ob), ('R', k)], writes=[('mix', sp)])
            state['released'] = True
            for _ in range(D2):
                yield 'delay'
            yield from wo_accum(sp, wi)

        PEND = []

        ON_SPAN_DONE = [None]

        def pend_advance(p):
            try:
                p['last'] = next(p['gen'])
                return True
            except StopIteration:
                PEND.remove(p)
                if p.get('hook'):
                    p['hook'](p['sp'])
                return False

        def pend_tick():
            first = True
            for p in list(PEND):
                if first or p['last'] == 'delay':
                    pend_advance(p)
                first = False

        def pend_drain(pred=lambda p: True, until_released=False):
            for p in list(PEND):
                if pred(p):
                    while p in PEND and not (until_released and p['state']['released']):
                        pend_advance(p)

        def pend_add(kind, sp, wi):
            pend_drain(lambda p: not p['state']['released'], until_released=True)
            pend_drain(lambda p: p['sp'] == sp)
            state = {'released': False}
            gen = post_diff(sp, wi, state) if kind == 'diff' else post_mla(sp, wi, state)
            p = {'gen': gen, 'sp': sp, 'wi': wi, 'state': state, 'last': None, 'hook': ON_SPAN_DONE[0]}
            PEND.append(p)
            pend_advance(p)

        def attention(kind, h, wi):
            diff = (kind == 'diff')
            nmaps = 2 if diff else 1
            scale = DIFF_SCALE if diff else MLA_SCALE
            steps = []
            for sp in range(4):
                for m in range(nmaps):
                    for kb in range(4 * sp + 4):
                        steps.append((sp, m, kb))

            def geom(st):
                sp, m, kb = st
                qb0 = 4 * sp
                first = max(kb, qb0)
                off = (first - qb0) * 128
                return qb0, first, off, 512 - off

            def emit_qk(i):
                sp, m, kb = steps[i]
                qb0, first, off, N = geom(steps[i])
                sbk = i % 3
                q0 = first * 128
                if diff:
                    if kb >= qb0:
                        nw, c0 = min(256, N), 0
                    elif kb == qb0 - 1:
                        nw, c0 = 128, 128
                    else:
                        nw, c0 = 0, 0
                else:
                    nw, c0 = (128, 0) if kb >= qb0 else (0, 0)

                def f(e):
                    if diff:
                        qm = Qs if m == 0 else ckvT
                        r = e.matmul(PS[sbk][:, 0:N], lhsT=Ks[:, kb * 128:(kb + 1) * 128],
                                     rhs=qm[:, q0:q0 + N], start=True, stop=(nw == 0))
                        if nw:
                            e.matmul(PS[sbk][:, 0:nw], lhsT=ident16[:, :], rhs=bt_hi[:, h, c0:c0 + nw], start=False, stop=False)
                            r = e.matmul(PS[sbk][:, 0:nw], lhsT=ident16[:, :], rhs=bt_lo[:, h, c0:c0 + nw], start=False, stop=True)
                    else:
                        e.matmul(PS[sbk][:, 0:N], lhsT=Ks[:, kb * 128:(kb + 1) * 128], rhs=Qs[:, q0:q0 + N],
                                 start=True, stop=False)
                        r = e.matmul(PS[sbk][:, 0:N], lhsT=krT[:, kb * 128:(kb + 1) * 128],
                                     rhs=QR[:, q0:q0 + N], start=False, stop=(nw == 0))
                        if nw:
                            r = e.matmul(PS[sbk][:, 0:nw], lhsT=ident16[:, :], rhs=mk16[:, 0:nw], start=False, stop=True)
                    return r
                rd = [('K', kb // 4), ('Q', sp) if (m == 0 or not diff) else ('ckv', sp)]
                if not diff:
                    rd += [('kr', kb // 4), ('QR', sp)]
                if nw:
                    rd += [('bt16',), ('ident16',)]
                SC.op('pe', f, reads=rd, writes=[('ps', sbk)])

            def emit_exp_av(i):
                sp, m, kb = steps[i]
                qb0, first, off, N = geom(steps[i])
                sbk, ei = i % 3, i % 4
                E = Eb[ei]
                if diff:
                    SC.op('act', lambda e: e.activation(out=E[:, 0:N], in_=PS[sbk][:, 0:N], func=AF.Exp, scale=scale),
                          reads=[('ps', sbk)], writes=[('E', ei)])
                    ob, smb = 3 + m, 5 + m
                else:
                    SC.op('act', lambda e: e.activation(out=E[:, 0:N], in_=PS[sbk][:, 0:N], func=AF.Exp, scale=scale),
                          reads=[('ps', sbk)], writes=[('E', ei)])
                    ob, smb = 3 + sp % 2, 5 + sp % 2
                last = 4 * sp + 3
                vcol = kb * 512 + h * 128

                def f(e):
                    e.matmul(PS[ob][:, off:512], lhsT=VU[:, vcol:vcol + 128], rhs=E[:, 0:N],
                             start=(kb == 0), stop=(kb == last))
                    return e.matmul(PS[smb][:, off:512], lhsT=ones16[:, :], rhs=E[:, 0:N],
                                    start=(kb == 0), stop=(kb == last))
                SC.op('pe', f, reads=[('VU', kb), ('ones16',), ('E', ei)], writes=[('ps', ob), ('ps', smb)])

            LA = 2
            deferred = []
            for i in range(min(LA, len(steps))):
                emit_qk(i)
            for i, st in enumerate(steps):
                if i + LA < len(steps):
                    emit_qk(i + LA)
                emit_exp_av(i)
                sp, m, kb = st
                for d in deferred:
                    d[0] -= 1
                while deferred and deferred[0][0] <= 0:
                    deferred.pop(0)[1]()
                if diff and kb == 4 * sp + 3:
                    def hp(m=m, sp=sp):
                        if m == 0:
                            pend_drain(lambda p: not p['state']['released'], until_released=True)
                        half_post(m)
                        if m == 1:
                            pend_add(kind, sp, wi)
                    deferred.append([2, hp])
                    pend_tick()
                elif (not diff) and kb == 4 * sp + 3:
                    pend_add(kind, sp, wi)
                else:
                    pend_tick()
            while deferred:
                deferred.pop(0)[1]()

        class LNPipe:
            def __init__(self, g_d, b_d, l, last):
                self.last = last
                self.q = []
                self.inflight = False
                SC.dma('sp', 'lng', lambda e: e.dma_start(out=lnG[:, :], in_=g_d[l, :].partition_broadcast(128)),
                       writes=[('lnG',)])
                SC.dma('sp', 'lnb', lambda e: e.dma_start(out=lnB[:, :], in_=b_d[l, :].partition_broadcast(128)),
                       writes=[('lnB',)])
                if not last:
                    SC.op('pool', lambda e: e.tensor_scalar(out=lnG[:, :], in0=lnG[:, :], scalar1=ALPHA, scalar2=1.0,
                                                            op0=ALU.mult, op1=ALU.mult),
                          reads=[('lnG',)], writes=[('lnG',)])
                    SC.op('pool', lambda e: e.tensor_scalar(out=lnB[:, :], in0=lnB[:, :], scalar1=ALPHA, scalar2=1.0,
                                                            op0=ALU.mult, op1=ALU.mult),
                          reads=[('lnB',)], writes=[('lnB',)])

            def stage_a1(self, t):
                xr = [('x', t, 0), ('x', t, 1)]
                k = t % 4
                cm, cr, cn = C_MV + 4 * k, C_RSTD + 4 * k, C_NMR + 4 * k
                SC.op('dve', lambda e: e.bn_stats(out=bnst[:, k * 12:k * 12 + 6], in_=x_tm[:, t, 0:512]),
                      reads=xr, writes=[('bn', k, 0)])
                SC.op('dve', lambda e: e.bn_stats(out=bnst[:, k * 12 + 6:k * 12 + 12], in_=x_tm[:, t, 512:1024]),
                      reads=xr, writes=[('bn', k, 1)])
                SC.op('dve', lambda e: e.bn_aggr(out=small[:, cm:cm + 2], in_=bnst[:, k * 12:k * 12 + 12]),
                      reads=[('bn', k, 0), ('bn', k, 1)], writes=[('sm', cm)])
                if self.inflight:
                    SC.op('act', lambda e: e.activation(out=col(cr), in_=small[:, cm + 1:cm + 2], func=AF.Ln,
                                                        bias=col(C_EPSL), scale=1.0),
                          reads=[('sm', cm), ('sm', C_EPSL)], writes=[('sm', cr)])
                    SC.op('act', lambda e: e.activation(out=col(cr), in_=col(cr), func=AF.Exp, scale=-0.5),
                          reads=[('sm', cr)], writes=[('sm', cr)])
                else:
                    SC.op('act', lambda e: e.activation(out=col(cr), in_=small[:, cm + 1:cm + 2], func=AF.Sqrt,
                                                        bias=col(C_EPSL), scale=1.0),
                          reads=[('sm', cm), ('sm', C_EPSL)], writes=[('sm', cr)])
                    SC.op('dve', lambda e: e.reciprocal(out=col(cr), in_=col(cr)), reads=[('sm', cr)], writes=[('sm', cr)])
                if not self.inflight:
                    SC.op('dve', lambda e: e.scalar_tensor_tensor(
                        out=col(cn), in0=small[:, cm:cm + 1], scalar=-1.0, in1=col(cr), op0=ALU.mult, op1=ALU.mult),
                        reads=[('sm', cm), ('sm', cr)], writes=[('sm', cn)])

            def stage_a2(self, t):
                xr = [('x', t, 0), ('x', t, 1)]
                k = t % 4
                cm, cr, cn = C_MV + 4 * k, C_RSTD + 4 * k, C_NMR + 4 * k
                if self.inflight:
                    SC.op('dve', lambda e: e.tensor_scalar(out=x_tm[:, t, :], in0=x_tm[:, t, :],
                                                           scalar1=small[:, cm:cm + 1], scalar2=col(cr),
                                                           op0=ALU.subtract, op1=ALU.mult),
                          reads=xr + [('sm', cm), ('sm', cr)], writes=xr)
                else:
                    SC.op('act', lambda e: e.activation(out=x_tm[:, t, :], in_=x_tm[:, t, :],
                                                        func=AF.Identity, bias=col(cn), scale=col(cr)),
                          reads=xr + [('sm', cr), ('sm', cn)], writes=xr)

            def stage_a3(self, t):
                xr = [('x', t, 0), ('x', t, 1)]
                SC.op('pool', lambda e: e.tensor_tensor(out=x_tm[:, t, :], in0=x_tm[:, t, :], in1=lnG[:, :], op=ALU.mult),
                      reads=xr + [('lnG',)], writes=xr)

            def stage_b(self, t):
                xr = [('x', t, 0), ('x', t, 1)]
                SC.op('dve', lambda e: e.tensor_tensor(out=x_tm[:, t, :], in0=x_tm[:, t, :], in1=lnB[:, :], op=ALU.add),
                      reads=xr + [('lnB',)], writes=xr)

            def stage_c(self, t):
                if self.last:
                    SC.dma('sp', 'out', lambda e: e.dma_start(out=out_d[t * 128:(t + 1) * 128, :], in_=x_tm[:, t, :]),
                           reads=[('x', t, 0), ('x', t, 1)])
                else:
                    make_xT(t, prescaled=True, inflight=self.inflight)

            def push(self, t):
                self.q.append(t)
                self._run(len(self.q) - 1)

            def _run(self, n):
                for s, fn in enumerate((self.stage_a1, self.stage_a2, self.stage_a3, self.stage_b, self.stage_c)):
                    i = n - s
                    if 0 <= i < len(self.q):
                        fn(self.q[i])

            def flush(self):
                n = len(self.q)
                for extra in range(4):
                    self._run(n + extra)

        def layer_norm(g_d, b_d, l, last):
            ln = LNPipe(g_d, b_d, l, last)
            for t in range(NT_):
                ln.push(t)
            ln.flush()

        def xT_blk(c, tb):
            return xT[:, c, tb * 512:(tb + 1) * 512]

        def proj_fm(lhs_fn, rhs_fn, nk, out_buf, okey, rkeys):
            for tb in range(4):
                b = ps_any()

                def f(e, tb=tb, b=b):
                    r = None
                    for c in range(nk):
                        r = e.matmul(PS[b][:, :], lhsT=lhs_fn(c), rhs=rhs_fn(c, tb), start=(c == 0), stop=(c == nk - 1))
                    return r
                SC.op('pe', f, reads=[rk(tb) if callable(rk) else rk for rk in rkeys], writes=[('ps', b)])
                evac(out_buf[:, tb * 512:(tb + 1) * 512], PS[b][:, :], [('ps', b)], [(okey, tb)])

        def load_wbig(src_fn, tag):
            for g in range(2):
                for (dst, s) in src_fn(g):
                    wload(dst, s, ('Wbig', g), 'wbig%d' % g)

        def wb4(g):
            return Wbig[:, g * 4:(g + 1) * 4, :]

        def load_wo(l, row0, wi):
            pend_drain(lambda p: p['wi'] == wi)
            wload(Wo2[wi][:, :], wo_d[l, row0:row0 + 128, :], ('Wo', wi), 'wo%d' % wi)

        def layer_consts(l):
            lambda_init = 0.8 - 0.6 * math.exp(-0.3 * l)
            for i, dsrc in enumerate((lq1_d, lk1_d, lq2_d, lk2_d)):
                const_dma(lamt[:, i, :], dsrc[l, :].partition_broadcast(128), ('lamt', i))
            SC.op('dve', lambda e: e.tensor_tensor(out=lamt[:, 0, :], in0=lamt[:, 0, :], in1=lamt[:, 1, :], op=ALU.mult),
                  reads=[('lamt', 0), ('lamt', 1)], writes=[('lamt', 0)])
            SC.op('dve', lambda e: e.tensor_tensor(out=lamt[:, 2, :], in0=lamt[:, 2, :], in1=lamt[:, 3, :], op=ALU.mult),
                  reads=[('lamt', 2), ('lamt', 3)], writes=[('lamt', 2)])
            SC.op('dve', lambda e: e.tensor_reduce(out=col(C_E1), in_=lamt[:, 0, :], axis=mybir.AxisListType.X, op=ALU.add),
                  reads=[('lamt', 0)], writes=[('sm', C_E1)])
            SC.op('dve', lambda e: e.tensor_reduce(out=col(C_E2), in_=lamt[:, 2, :], axis=mybir.AxisListType.X, op=ALU.add),
                  reads=[('lamt', 2)], writes=[('sm', C_E2)])
            SC.op('act', lambda e: e.activation(out=col(C_E1), in_=col(C_E1), func=AF.Exp),
                  reads=[('sm', C_E1)], writes=[('sm', C_E1)])
            SC.op('act', lambda e: e.activation(out=col(C_E2), in_=col(C_E2), func=AF.Exp),
                  reads=[('sm', C_E2)], writes=[('sm', C_E2)])
            SC.op('dve', lambda e: e.tensor_tensor(out=col(C_NLAM), in0=col(C_E2), in1=col(C_E1), op=ALU.subtract),
                  reads=[('sm', C_E1), ('sm', C_E2)], writes=[('sm', C_NLAM)])
            SC.op('dve', lambda e: e.tensor_scalar(out=col(C_NLAM), in0=col(C_NLAM), scalar1=-lambda_init,
                                                   scalar2=None, op0=ALU.add),
                  reads=[('sm', C_NLAM)], writes=[('sm', C_NLAM)])
            const_dma(col(C_GS), subg_d[l, :].rearrange("(p o) -> p o", o=1), ('sm', C_GS))
            SC.op('dve', lambda e: e.tensor_scalar(out=col(C_GS), in0=col(C_GS), scalar1=1.0 - lambda_init,
                                                   scalar2=None, op0=ALU.mult),
                  reads=[('sm', C_GS)], writes=[('sm', C_GS)])
            const_dma(col(C_GKV), kvg_d[l, :].rearrange("(p o) -> p o", o=1), ('sm', C_GKV))
            const_dma(col(C_GQ0), qg_d[l, 0:128].rearrange("(p o) -> p o", o=1), ('sm', C_GQ0))
            const_dma(col(C_GQ1), qg_d[l, 128:256].rearrange("(p o) -> p o", o=1), ('sm', C_GQ1))

        def load_v_weights(l):
            load_wbig(lambda g: [(wb4(g), w_in_d[l, g * 512:(g + 1) * 512, 1024:1536].rearrange("(c p) n -> p c n", p=128))], 'v')

        def load_wqk(l, h):
            for part, c0 in ((0, h * 128), (1, 512 + h * 128)):
                wload(Wqk[:, :, part * 128:(part + 1) * 128],
                      w_in_d[l, :, c0:c0 + 128].rearrange("(c p) n -> p c n", p=128), ('Wqk', part), 'wqk%d' % part)

        def diff_v(l):
            for t in range(NT_):
                b = ps_any()

                def f(e, t=t, b=b):
                    r = None
                    for c in range(8):
                        r = e.matmul(PS[b][:, :], lhsT=xT[:, c, t * 128:(t + 1) * 128], rhs=Wbig[:, c, :],
                                     start=(c == 0), stop=(c == 7))
                    return r
                SC.op('pe', f, reads=[('xT', t // 4), ('Wbig', 0), ('Wbig', 1)], writes=[('ps', b)])
                evac(VU[:, t * 512:(t + 1) * 512], PS[b][:, :], [('ps', b)], [('VU', t)])

        def diff_head(l, h):
            for tb in range(4):
                b = ps_any()

                def f(e, tb=tb, b=b):
                    r = None
                    for c in range(8):
                        r = e.matmul(PS[b][:, :], lhsT=Wqk[:, c, 0:128], rhs=xT_blk(c, tb), start=(c == 0), stop=(c == 7))
                    return r
                SC.op('pe', f, reads=[('xT', tb), ('Wqk', 0)], writes=[('ps', b)])
                SC.op('act', lambda e, tb=tb, b=b: e.copy(out=Qs[0:64, tb * 512:(tb + 1) * 512], in_=PS[b][0:64, :]),
                      reads=[('ps', b)], writes=[('Q', tb)])
                SC.op('dve', lambda e, tb=tb, b=b: e.tensor_copy(out=ckvT[64:128, tb * 512:(tb + 1) * 512], in_=PS[b][64:128, :]),
                      reads=[('ps', b)], writes=[('ckv', tb)])
            proj_fm(lambda c: Wqk[:, c, 128:256], xT_blk, 8, Ks, 'K', [lambda tb: ('xT', tb), ('Wqk', 1)])
            if h < 3:
                load_wqk(l, h + 1)
                load_wo(l, (h + 1) * 128, (h + 1) % 3)
            else:
                load_wo(l, 512, 4 % 3)
            attention('diff', h, h % 3)

        def load_mla_win(l):
            load_wbig(lambda g: [
                (wb4(g)[:, :, 0:448], w_in_d[l, g * 512:(g + 1) * 512, 1536:1984].rearrange("(c p) n -> p c n", p=128)),
                (wb4(g)[:, :, 448:512], krsw_d[l, g * 512:(g + 1) * 512, :].rearrange("(c p) n -> p c n", p=128))], 'm')

        def load_mla_small(l):
            wukv4 = wukv_d[l, :, :].rearrange("p (h n) -> p h n", h=4)
            wload(Wkn[:, :].rearrange("p (h n) -> p h n", h=4), wukv4[:, :, 0:128], ('Wkn',), 'wkn')
            wload(Wmv[:, :].rearrange("p (h n) -> p h n", h=4), wukv4[:, :, 128:256], ('Wmv',), 'wmv')
            for j in range(2):
                wload(Wuq[:, j, 0:768], wuq_d[l, j * 128:(j + 1) * 128, :], ('Wuq', j), 'wuq%d' % j)
                wload(Wuq[:, j, 768:1024], wuqsw_d[l, j * 128:(j + 1) * 128, :], ('Wuq', j), 'wuq%d' % j)

        def rms_block(banks, nfeat, rk, out_fn, okeys, gcols):
            n = len(banks)
            for j, b in enumerate(banks):
                SC.op('act', lambda e, j=j, b=b: e.activation(out=Tb[j][:, :], in_=PS[b][:, :], func=AF.Square),
                      reads=[('ps', b)], writes=[('T', j)])
            bs = ps_any()

            def f(e):
                r = None
                for j in range(n):
                    r = e.matmul(PS[bs][:, :], lhsT=ones32[:, :], rhs=Tb[j][:, :], start=(j == 0), stop=(j == n - 1))
                return r
            SC.op('pe', f, reads=[('T', j) for j in range(n)] + [('ones32',)], writes=[('ps', bs)])
            SC.op('act', lambda e: e.activation(out=Rb[rk][:, :], in_=PS[bs][:, :], func=AF.Ln,
                                                bias=col(C_EPSR), scale=1.0 / nfeat),
                  reads=[('ps', bs), ('sm', C_EPSR)], writes=[('R', rk)])
            SC.op('act', lambda e: e.activation(out=Rb[rk][:, :], in_=Rb[rk][:, :], func=AF.Exp, scale=-0.5),
                  reads=[('R', rk)], writes=[('R', rk)])
            for j, b in enumerate(banks):
                SC.op('dve', lambda e, j=j, b=b: e.scalar_tensor_tensor(out=out_fn(j), in0=PS[b][:, :], scalar=col(gcols[j]),
                                                                        in1=Rb[rk][:, :], op0=ALU.mult, op1=ALU.mult),
                      reads=[('ps', b), ('R', rk), ('sm', gcols[j])], writes=[okeys[j]])

        def mm8(b, c0, w, tb, m=128):
            def f(e):
                r = None
                for c in range(8):
                    r = e.matmul(PS[b][0:m, :], lhsT=Wbig[:, c, c0:c0 + w], rhs=xT_blk(c, tb), start=(c == 0), stop=(c == 7))
                return r
            SC.op('pe', f, reads=[('xT', tb), ('Wbig', 0), ('Wbig', 1)], writes=[('ps', b)])

        def mla_pre(tb):
            tsl = slice(tb * 512, (tb + 1) * 512)
            bq = [ps_any(), ps_any()]
            for j in range(2):
                mm8(bq[j], j * 128, 128, tb)
            rms_block(bq, 256.0, 0, lambda j: cqT[:, j, tsl], [('CW', tb), ('CW', 4 + tb)], [C_GQ0, C_GQ1])
            bk = ps_any()
            mm8(bk, 256, 128, tb)
            rms_block([bk], 128.0, 1, lambda j: ckvT[:, tsl], [('ckv', tb)], [C_GKV])
            ba, bb = ps_any(), ps_any()
            mm8(ba, 384, 64, tb, m=64)
            mm8(bb, 448, 64, tb, m=64)
            rope_block(ba, bb, tb, krT[0:64, tsl], ('kr', tb))

        def mla_mv():
            for t in range(NT_):
                b = ps_any()
                SC.op('pe', lambda e, t=t, b=b: e.matmul(PS[b][:, :], lhsT=ckvT[:, t * 128:(t + 1) * 128], rhs=Wmv[:, :],
                                                         start=True, stop=True),
                      reads=[('ckv', t // 4), ('Wmv',)], writes=[('ps', b)])
                evac(VU[:, t * 512:(t + 1) * 512], PS[b][:, :], [('ps', b)], [('VU', t)])

        def mla_head(l, h):
            proj_fm(lambda c: Wkn[:, h * 128:(h + 1) * 128], lambda c, tb: ckvT[:, tb * 512:(tb + 1) * 512], 1, Ks, 'K',
                    [lambda tb: ('ckv', tb), ('Wkn',)])
            proj_fm(lambda c: Wuq[:, c, h * 192:h * 192 + 128], lambda c, tb: cqT[:, c, tb * 512:(tb + 1) * 512], 2, Qs, 'Q',
                    [lambda tb: ('CW', tb), lambda tb: ('CW', 4 + tb), ('Wuq', 0), ('Wuq', 1)])

            def qr_blk(tb):
                tsl = slice(tb * 512, (tb + 1) * 512)
                ba, bb = ps_any(), ps_any()
                for (b, c0) in ((ba, h * 192 + 128), (bb, 768 + h * 64)):
                    def f(e, b=b, c0=c0):
                        e.matmul(PS[b][0:64, :], lhsT=Wuq[:, 0, c0:c0 + 64], rhs=cqT[:, 0, tsl], start=True, stop=False)
                        return e.matmul(PS[b][0:64, :], lhsT=Wuq[:, 1, c0:c0 + 64], rhs=cqT[:, 1, tsl],
                                        start=False, stop=True)
                    SC.op('pe', f, reads=[('CW', tb), ('CW', 4 + tb), ('Wuq', 0), ('Wuq', 1)], writes=[('ps', b)])
                rope_block(ba, bb, tb, QR[0:64, tsl], ('QR', tb))
            for tb in range(4):
                qr_blk(tb)
            if h < 3:
                load_wo(l, 512 + (h + 1) * 128, (4 + h + 1) % 3)
            else:
                load_w2(l, 0)
            attention('mla', h, (4 + h) % 3)

        def load_w1(l, fg):
            load_wbig(lambda g: [(wb4(g), w1_d[l, g * 512:(g + 1) * 512, fg * 512:(fg + 1) * 512].rearrange("(c p) n -> p c n", p=128))], 'w1')

        def load_w2(l, fg):
            for g in range(2):
                r0 = fg * 512 + g * 256
                SC.dma('pool', 'w2_%d' % g, lambda e, g=g, r0=r0: e.dma_start(
                    out=W2[:, g * 2:(g + 1) * 2, :], in_=w2_d[l, r0:r0 + 256, :].rearrange("(c p) n -> p c n", p=128)),
                    writes=[('CW', 4 * g + i) for i in range(4)])

        def mlp_u(j, tb):
            b = ps_any()
            mm8(b, j * 128, 128, tb)
            k = (j * 4 + tb) % 2
            SC.op('act', lambda e: e.activation(out=Tb[k][:, :], in_=PS[b][:, :], func=AF.Relu),
                  reads=[('ps', b)], writes=[('T', k)])
            uc = j * 2048 + tb * 512
            SC.op('act', lambda e: e.activation(out=VU[:, uc:uc + 512], in_=Tb[k][:, :], func=AF.Square),
                  reads=[('T', k)], writes=[('VU', j * 4 + tb)])

        def mlp_group(l, fg, ln=None, u_done=False):
            if not u_done:
                for j in range(4):
                    for tb in range(4):
                        mlp_u(j, tb)
            if fg < 7:
                load_w1(l, fg + 1)
            elif l + 1 < DEPTH and STOP is None:
                load_v_weights(l + 1)
            for t in range(NT_):
                for hf in range(2):
                    b = ps_any()

                    def f(e, t=t, hf=hf, b=b):
                        r = None
                        for j in range(4):
                            uc = j * 2048 + t * 128
                            r = e.matmul(PS[b][:, :], lhsT=VU[:, uc:uc + 128], rhs=W2[:, j, hf * 512:(hf + 1) * 512],
                                         start=(j == 0), stop=(j == 3))
                        return r
                    SC.op('pe', f, reads=[('VU', j * 4 + t // 4) for j in range(4)] + [('CW', i) for i in range(8)],
                          writes=[('ps', b)])
                    SC.op('dve', lambda e, t=t, hf=hf, b=b: e.tensor_tensor(
                        out=x_tm[:, t, hf * 512:(hf + 1) * 512], in0=PS[b][:, :],
                        in1=x_tm[:, t, hf * 512:(hf + 1) * 512], op=ALU.add),
                        reads=[('ps', b), ('x', t, hf)], writes=[('x', t, hf)])
                if ln is not None:
                    ln.push(t)
            if fg < 7:
                load_w2(l, fg + 1)

        for l in range(DEPTH):
            layer_consts(l)
            if STOP == 'xT':
                break
            if l == 0:
                load_v_weights(l)
            load_wqk(l, 0)
            load_wo(l, 0, 0)
            load_mla_small(l)
            SC.op('pool', lambda e: e.memset(Qs[64:128, :], 0.0), writes=[('Q', tb) for tb in range(4)])
            SC.op('pool', lambda e: e.memset(ckvT[0:64, :], 0.0), writes=[('ckv', tb) for tb in range(4)])
            diff_v(l)
            load_mla_win(l)
            for h in range(4):
                diff_head(l, h)
            pend_drain()
            if STOP == 'diff':
                break
            for tb in range(4):
                mla_pre(tb)
            load_w1(l, 0)
            mla_mv()
            for h in range(3):
                mla_head(l, h)
            if STOP in ('attn', 'ln1'):
                mla_head(l, 3)
                pend_drain()
                if STOP == 'attn':
                    break
                layer_norm(ln1g_d, ln1b_d, l, True)
                break
            mla_head(l, 3)
            pend_drain()
            layer_norm(ln1g_d, ln1b_d, l, False)
            ln2 = LNPipe(ln2g_d, ln2b_d, l, l == DEPTH - 1 or STOP == 'l0')
            for fg in range(8):
                mlp_group(l, fg, ln2 if fg == 7 else None)
            ln2.flush()
            if STOP == 'l0':
                break
        if STOP in ('xT', 'diff', 'attn'):
            for t in range(NT_):
                SC.dma('sp', 'out', lambda e, t=t: e.dma_start(out=out_d[t * 128:(t + 1) * 128, :], in_=x_tm[:, t, :]),
                       reads=[('x', t, 0), ('x', t, 1)])

        sem_keys = list(SC.cnt.keys())
        sems = {k: es.enter_context(nc.semaphore("s_%s" % str(k))) for k in sem_keys}
        out_total = SC.cnt['out']
        with nc.Block() as block:
            @block.tensor
            def _(eng):
                SC.replay('pe', eng, sems)

            @block.scalar
            def _(eng):
                SC.replay('act', eng, sems)

            @block.vector
            def _(eng):
                SC.replay('dve', eng, sems)

            @block.gpsimd
            def _(eng):
                SC.replay('pool', eng, sems)

            @block.sync
            def _(eng):
                SC.replay('sp', eng, sems)
                eng.wait_ge(sems['out'], out_total)
    return nc


def _host_consts():
    k = np.arange(128)[:, None]
    j = np.arange(256)[None, :]
    rel = k - j
    nb = 16
    ret = (rel > 0).astype(np.int32) * nb
    n = np.abs(rel)
    max_exact = nb // 2
    nf = np.maximum(n, 1).astype(np.float32)
    large = max_exact + (np.log(nf / max_exact) / math.log(128 / max_exact) * (nb - max_exact)).astype(np.int32)
    large = np.minimum(large, nb - 1)
    bucket = ret + np.where(n < max_exact, n, large)
    masked = (k >= 64) & (j < 64)
    pos = np.arange(S, dtype=np.float32)
    inv = (10000.0 ** (-np.arange(0, 64, 2, dtype=np.float32) / 64)).astype(np.float32)
    ang = pos[None, :] * inv[:, None]
    cos = np.cos(ang).astype(np.float32)
    sin = np.sin(ang).astype(np.float32)
    cos2 = np.concatenate([cos, cos], axis=0)
    sin2s = np.concatenate([-sin, sin], axis=0)
    mask_tile = np.where((np.arange(128)[:, None] >= 64) & (np.arange(128)[None, :] < 64), NEG, 0.0).astype(np.float32)
    return bucket, masked, np.ascontiguousarray(cos2), np.ascontiguousarray(sin2s), mask_tile


_NC_CACHE = {}


def kernel(x, w_in, lambda_q1, lambda_k1, lambda_q2, lambda_k2, subln_g, q_norm_g, w_uq, kv_norm_g,
           w_ukv, rel_bias, w_o, ln1_g, ln1_b, w_mlp_in, w_mlp_out, ln2_g, ln2_b):
    f = lambda a: np.ascontiguousarray(np.asarray(a, dtype=np.float32))
    x = f(x)
    w_in = f(w_in)
    w_uq = f(w_uq)
    rel_bias = f(rel_bias)
    bucket, masked, cos2, sin2s, mask_tile = _host_consts()
    kr = w_in[:, :, 1920:1984]
    krsw = np.ascontiguousarray(np.concatenate([kr[:, :, 32:64], kr[:, :, 0:32]], axis=2))
    uq4 = w_uq.reshape(DEPTH, 256, 4, 192)[:, :, :, 128:192]
    uqsw = np.ascontiguousarray(np.concatenate([uq4[..., 32:64], uq4[..., 0:32]], axis=3).reshape(DEPTH, 256, 256))
    bt = np.transpose(rel_bias[bucket], (2, 0, 1))
    bt = np.ascontiguousarray(np.where(masked[None], np.float32(NEG), bt).astype(np.float32))
    shared = {
        "w_in": w_in, "w_in_krsw": krsw, "lambda_q1": f(lambda_q1), "lambda_k1": f(lambda_k1),
        "lambda_q2": f(lambda_q2), "lambda_k2": f(lambda_k2), "subln_g": f(subln_g), "q_norm_g": f(q_norm_g),
        "w_uq": w_uq, "w_uq_sw": uqsw, "kv_norm_g": f(kv_norm_g), "w_ukv": f(w_ukv), "rel_bias": rel_bias,
        "bias_tiles": bt, "mask_tile": mask_tile, "w_o": f(w_o), "ln1_g": f(ln1_g), "ln1_b": f(ln1_b),
        "w_mlp_in": f(w_mlp_in), "w_mlp_out": f(w_mlp_out), "ln2_g": f(ln2_g), "ln2_b": f(ln2_b),
        "cos2T": cos2, "sin2sT": sin2s, "ident": np.eye(128, dtype=np.float32),
    }
    if 'nc' not in _NC_CACHE:
        _NC_CACHE['nc'] = build_program()
    nc = _NC_CACHE['nc']
    in_maps = []
    for b in range(8):
        m = dict(shared)
        m["x"] = np.ascontiguousarray(x[b])
        in_maps.append(m)
    res = run_bass_kernel_spmd(nc, in_maps, core_ids=list(range(8)))
    return np.stack([np.asarray(r["out"], dtype=np.float32) for r in res.results], axis=0)
```
